# Optimizing a Trainium2 kernel written in Bass

```python
import jax
import jax.numpy as jnp
from jax import lax
import numpy as np

D_MODEL = 1024
BATCH = 8
SEQ = 4096
DEPTH = 4

GRID_W = 64
CTX_LEN = 256
EPS = 1e-6
HEAD_DIM = 64
ROPE_BASE = 10000.0

HG_HEADS = 4
HG_DK = 64
HG_DV = 64
HG_QK = HG_HEADS * HG_DK
HG_WIDTH = HG_HEADS * HG_DV
HG_CHUNK = 16

ATT_HEADS = 8
ATT_KV_HEADS = 2
ATT_GROUP = ATT_HEADS // ATT_KV_HEADS
ATT_WIDTH = ATT_HEADS * HEAD_DIM
ATT_KV_WIDTH = ATT_KV_HEADS * HEAD_DIM
WINDOW = 128
ATT_BLOCK = 128

LRU_WIDTH = 256
LRU_BLOCKS = 4
LRU_BLOCK = LRU_WIDTH // LRU_BLOCKS
LRU_C = 8.0
LRU_CONV = 4
LRU_CONV_LEFT = 2

D_FF = 2816
FFN_CONV = 3

MIX_WIDTH = HG_WIDTH + ATT_WIDTH + LRU_WIDTH
IN_SPLITS = (HG_QK, HG_QK, HG_QK, HG_WIDTH, HG_WIDTH, ATT_WIDTH, ATT_KV_WIDTH, ATT_KV_WIDTH, LRU_WIDTH, LRU_WIDTH)
IN_WIDTH = sum(IN_SPLITS)

kernel_name = 'hybrid_hgrn2_swa_rglru_dit_block'


def rmsnorm(x, g):
    xf = x.astype(jnp.float32)
    y = xf * lax.rsqrt(jnp.mean(xf * xf, axis=-1, keepdims=True) + EPS)
    return (y * g.astype(jnp.float32)).astype(x.dtype)


def modulate(x, g, shift, scale):
    return rmsnorm(x, g) * (1 + jnp.expand_dims(scale, -2)) + jnp.expand_dims(shift, -2)


def dwconv(x, w, left):
    k, ch = w.shape
    return lax.conv_general_dilated(x, w[:, None, :], window_strides=(1,), padding=[(left, k - 1 - left)],
                                    dimension_numbers=('NWC', 'WIO', 'NWC'), feature_group_count=ch)


def flip_if(t, rev):
    return jnp.flip(t, axis=1) if rev else t


def rope_2d(rows):
    n_freq = HEAD_DIM // 4
    inv = ROPE_BASE ** (-jnp.arange(n_freq, dtype=jnp.float32) / n_freq)
    r = jnp.repeat(jnp.arange(rows, dtype=jnp.float32), GRID_W)
    col = jnp.tile(jnp.arange(GRID_W, dtype=jnp.float32), rows)
    ang = jnp.concatenate([r[:, None] * inv, col[:, None] * inv], axis=-1)
    return jnp.cos(ang), jnp.sin(ang)


def apply_rope(x, cos, sin):
    xf = x.astype(jnp.float32)
    half = HEAD_DIM // 2
    x1, x2 = xf[..., :half], xf[..., half:]
    cs, sn = cos[None, :, None, :], sin[None, :, None, :]
    return jnp.concatenate([x1 * cs - x2 * sn, x2 * cs + x1 * sn], axis=-1).astype(x.dtype)


def hgrn2_gates(z, lb):
    zf = z.astype(jnp.float32)
    log_f = jnp.log(lb + (1.0 - lb) * jax.nn.sigmoid(zf))
    k = (1.0 - lb) * jax.nn.sigmoid(-zf)
    return log_f, k


def hgrn2_chunkwise(q, k, v, log_f, s0):
    b, l, h, _ = q.shape
    n = l // HG_CHUNK

    def chunks(t):
        return t.reshape(b, n, HG_CHUNK, h, t.shape[-1])

    q, k, v, log_f = chunks(q), chunks(k), chunks(v), chunks(log_f)
    a = jnp.cumsum(log_f, axis=2)
    a_end = a[:, :, -1:]
    lower = jnp.tril(jnp.ones((HG_CHUNK, HG_CHUNK), dtype=bool))[:, :, None, None]
    rel = jnp.where(lower, a[:, :, :, None] - a[:, :, None, :], -jnp.inf)
    scores = jnp.einsum('bnthd,bntshd->bnhts', q, jnp.exp(rel) * k[:, :, None])
    o_intra = jnp.einsum('bnhts,bnshe->bnthe', scores, v)
    q_in = q * jnp.exp(a)
    k_out = k * jnp.exp(a_end - a)
    decay = jnp.exp(a_end[:, :, 0])

    def step(s, xs):
        qc, kc, vc, dc = xs
        o_c = jnp.einsum('bthd,bhde->bthe', qc, s)
        s = dc[..., None] * s + jnp.einsum('bthd,bthe->bhde', kc, vc)
        return s, o_c

    xs = (jnp.moveaxis(q_in, 1, 0), jnp.moveaxis(k_out, 1, 0), jnp.moveaxis(v, 1, 0), jnp.moveaxis(decay, 1, 0))
    s_end, o_inter = lax.scan(step, s0, xs)
    o = o_intra + jnp.moveaxis(o_inter, 0, 1)
    return o.reshape(b, l, h, -1), s_end


def hgrn2_final_state(k, v, log_f):
    a = jnp.cumsum(log_f, axis=1)
    return jnp.einsum('blhd,blhe->bhde', k * jnp.exp(a[:, -1:] - a), v)


def hgrn2_mixer(lat, ctx, lb, norm_g, want_ctx):
    lb = lb.reshape(HG_HEADS, HG_DK).astype(jnp.float32)

    def heads(t):
        return t.reshape(t.shape[0], t.shape[1], HG_HEADS, -1).astype(jnp.float32)

    def readout(o, g):
        o = o * lax.rsqrt(jnp.mean(o * o, axis=-1, keepdims=True) + EPS)
        o = o.reshape(o.shape[0], o.shape[1], HG_WIDTH) * norm_g.astype(jnp.float32)
        return (o * jax.nn.silu(g.astype(jnp.float32))).astype(g.dtype)

    q_l, ff_l, fb_l, i_l, g_l = lat
    q_c, ff_c, fb_c, i_c, g_c = ctx
    b = q_l.shape[0]
    q_l = jax.nn.silu(heads(q_l))
    v_l = heads(i_l)
    v_c = heads(i_c)
    if want_ctx:
        q_c = jax.nn.silu(heads(q_c))
    zeros = jnp.zeros((b, HG_HEADS, HG_DK, HG_DV), jnp.float32)
    o_lat = 0.0
    o_ctx = 0.0
    for f_l, f_c, rev in ((ff_l, ff_c, False), (fb_l, fb_c, True)):
        lf_c, k_c = hgrn2_gates(flip_if(heads(f_c), rev), lb)
        if want_ctx:
            oc, s_c = hgrn2_chunkwise(flip_if(q_c, rev), k_c, flip_if(v_c, rev), lf_c, zeros)
            o_ctx = o_ctx + flip_if(oc, rev)
        else:
            s_c = hgrn2_final_state(k_c, flip_if(v_c, rev), lf_c)
        lf_l, k_l = hgrn2_gates(flip_if(heads(f_l), rev), lb)
        ol, _ = hgrn2_chunkwise(flip_if(q_l, rev), k_l, flip_if(v_l, rev), lf_l, s_c)
        o_lat = o_lat + flip_if(ol, rev)
    y_lat = readout(o_lat, g_l)
    y_ctx = readout(o_ctx, g_c) if want_ctx else None
    return y_lat, y_ctx


def window_attention(q_l, k_l, v_l, q_c, k_c, v_c, sink, cos, sin, want_ctx):
    f32 = jnp.float32
    b, l, _ = q_l.shape
    nb = l // ATT_BLOCK
    scale = HEAD_DIM ** -0.5
    q_l = apply_rope(q_l.reshape(b, l, ATT_HEADS, HEAD_DIM), cos, sin)
    k_l = apply_rope(k_l.reshape(b, l, ATT_KV_HEADS, HEAD_DIM), cos, sin)
    v_l = v_l.reshape(b, l, ATT_KV_HEADS, HEAD_DIM)
    k_c = k_c.reshape(b, -1, ATT_KV_HEADS, HEAD_DIM)
    v_c = v_c.reshape(b, -1, ATT_KV_HEADS, HEAD_DIM)
    sink = sink.astype(f32).reshape(ATT_KV_HEADS, ATT_GROUP, 1, 1)

    qb = q_l.reshape(b, nb, ATT_BLOCK, ATT_KV_HEADS, ATT_GROUP, HEAD_DIM)

    def band(t):
        tp = jnp.pad(t, ((0, 0), (ATT_BLOCK, ATT_BLOCK), (0, 0), (0, 0)))
        tp = tp.reshape(b, nb + 2, ATT_BLOCK, ATT_KV_HEADS, HEAD_DIM)
        return jnp.concatenate([tp[:, :-2], tp[:, 1:-1], tp[:, 2:]], axis=2)

    kw, vw = band(k_l), band(v_l)
    q_pos = jnp.arange(l).reshape(nb, ATT_BLOCK)
    k_pos = (jnp.arange(nb)[:, None] - 1) * ATT_BLOCK + jnp.arange(3 * ATT_BLOCK)[None, :]
    valid = ((jnp.abs(q_pos[:, :, None] - k_pos[:, None, :]) <= WINDOW)
             & (k_pos >= 0)[:, None, :] & (k_pos < l)[:, None, :])
    s_loc = jnp.einsum('bnqhgd,bnkhd->bnhgqk', qb, kw, preferred_element_type=f32) * scale
    s_loc = jnp.where(valid[None, :, None, None], s_loc, -jnp.inf)
    s_ctx = jnp.einsum('bnqhgd,bchd->bnhgqc', qb, k_c, preferred_element_type=f32) * scale
    m = jnp.maximum(jnp.maximum(s_loc.max(-1, keepdims=True), s_ctx.max(-1, keepdims=True)), sink)
    e_loc = jnp.exp(s_loc - m)
    e_ctx = jnp.exp(s_ctx - m)
    denom = e_loc.sum(-1) + e_ctx.sum(-1) + jnp.exp(sink - m)[..., 0]
    o = (jnp.einsum('bnhgqk,bnkhd->bnhgqd', e_loc, vw.astype(f32))
         + jnp.einsum('bnhgqc,bchd->bnhgqd', e_ctx, v_c.astype(f32)))
    o = o / denom[..., None]
    y_lat = jnp.transpose(o, (0, 1, 4, 2, 3, 5)).reshape(b, l, ATT_WIDTH).astype(q_l.dtype)

    y_ctx = None
    if want_ctx:
        qc = q_c.reshape(b, -1, ATT_KV_HEADS, ATT_GROUP, HEAD_DIM)
        s = jnp.einsum('bqhgd,bkhd->bhgqk', qc, k_c, preferred_element_type=f32) * scale
        mc = jnp.maximum(s.max(-1, keepdims=True), sink)
        e = jnp.exp(s - mc)
        oc = jnp.einsum('bhgqk,bkhd->bhgqd', e, v_c.astype(f32))
        oc = oc / (e.sum(-1) + jnp.exp(sink - mc)[..., 0])[..., None]
        y_ctx = jnp.transpose(oc, (0, 3, 1, 2, 4)).reshape(b, -1, ATT_WIDTH).astype(q_c.dtype)
    return y_lat, y_ctx


def rglru_coeffs(x, w_r, b_r, w_i, b_i, lam):
    f32 = jnp.float32
    xb = x.reshape(x.shape[0], x.shape[1], LRU_BLOCKS, LRU_BLOCK)
    r = jax.nn.sigmoid(jnp.einsum('blnd,nde->blne', xb, w_r.astype(f32)).reshape(x.shape) + b_r.astype(f32))
    i = jax.nn.sigmoid(jnp.einsum('blnd,nde->blne', xb, w_i.astype(f32)).reshape(x.shape) + b_i.astype(f32))
    log_a = -LRU_C * r * jax.nn.softplus(-lam.astype(f32))
    u = jnp.sqrt(-jnp.expm1(2.0 * log_a)) * (i * x)
    return log_a, u


def linear_scan(log_a, u, h0):
    def combine(left, right):
        return left[0] * right[0], right[0] * left[1] + right[1]

    a_cum, h = lax.associative_scan(combine, (jnp.exp(log_a), u), axis=1)
    return h + a_cum * h0[:, None]


def lru_final_state(log_a, u):
    cl = jnp.cumsum(log_a, axis=1)
    return jnp.sum(jnp.exp(cl[:, -1:] - cl) * u, axis=1)


def rglru_mixer(x_l, y_l, x_c, y_c, conv_w, conv_b, w_r, b_r, w_i, b_i, lam, want_ctx):
    xl = (dwconv(x_l, conv_w, LRU_CONV_LEFT) + conv_b).astype(jnp.float32)
    xc = (dwconv(x_c, conv_w, LRU_CONV_LEFT) + conv_b).astype(jnp.float32)
    zeros = jnp.zeros((xl.shape[0], LRU_WIDTH), jnp.float32)
    h_l = 0.0
    h_c = 0.0
    for d in range(2):
        rev = d == 1
        la_c, u_c = rglru_coeffs(flip_if(xc, rev), w_r[d], b_r[d], w_i[d], b_i[d], lam[d])
        if want_ctx:
            hc = linear_scan(la_c, u_c, zeros)
            s_c = hc[:, -1]
            h_c = h_c + flip_if(hc, rev)
        else:
            s_c = lru_final_state(la_c, u_c)
        la_l, u_l = rglru_coeffs(flip_if(xl, rev), w_r[d], b_r[d], w_i[d], b_i[d], lam[d])
        h_l = h_l + flip_if(linear_scan(la_l, u_l, s_c), rev)
    y_lat = (jax.nn.gelu(y_l.astype(jnp.float32)) * h_l).astype(y_l.dtype)
    y_ctx = (jax.nn.gelu(y_c.astype(jnp.float32)) * h_c).astype(y_c.dtype) if want_ctx else None
    return y_lat, y_ctx


def split_proj(p):
    idx = np.cumsum(IN_SPLITS)[:-1].tolist()
    return jnp.split(p, idx, axis=-1)


def token_mixer(h_l, h_c, w_in, w_out, lb, hg_norm_g, sink, conv_w, conv_b, w_r, b_r, w_i, b_i, lam,
                cos, sin, want_ctx):
    pl = split_proj(h_l @ w_in)
    pc = split_proj(h_c @ w_in)
    hg_l, hg_c = hgrn2_mixer(pl[0:5], pc[0:5], lb, hg_norm_g, want_ctx)
    at_l, at_c = window_attention(pl[5], pl[6], pl[7], pc[5], pc[6], pc[7], sink, cos, sin, want_ctx)
    lr_l, lr_c = rglru_mixer(pl[8], pl[9], pc[8], pc[9], conv_w, conv_b, w_r, b_r, w_i, b_i, lam, want_ctx)
    y_l = jnp.concatenate([hg_l, at_l, lr_l], axis=-1) @ w_out
    y_c = jnp.concatenate([hg_c, at_c, lr_c], axis=-1) @ w_out if want_ctx else None
    return y_l, y_c


def conv_ffn(h, w_up, conv_w, conv_b, w_down):
    u = dwconv(h @ w_up, conv_w, FFN_CONV // 2) + conv_b
    gate, val = jnp.split(u, 2, axis=-1)
    return (jax.nn.silu(gate) * val) @ w_down


def setup_inputs(seed: int = 0) -> dict:
    key = jax.random.key(seed)
    ks = jax.random.split(key, 26)
    f32 = jnp.float32

    def nrm(k, shape, scale):
        return jax.random.normal(k, shape, f32) * scale

    d = D_MODEL
    u = jax.random.uniform(ks[18], (DEPTH, 2, LRU_WIDTH), f32, 0.9, 0.999)
    a_base = u ** (1.0 / LRU_C)
    return {
        'x': nrm(ks[0], (BATCH, SEQ, d), 1.0),
        'c': nrm(ks[1], (BATCH, d), 1.0),
        'ctx': nrm(ks[2], (BATCH, CTX_LEN, d), 1.0),
        'c_ctx': nrm(ks[3], (d,), 1.0),
        'ada_w': nrm(ks[4], (DEPTH, d, 6 * d), 0.5 * d ** -0.5),
        'ada_b': nrm(ks[5], (DEPTH, 6 * d), 0.01),
        'norm_mix_g': 1.0 + nrm(ks[6], (DEPTH, d), 0.1),
        'norm_ffn_g': 1.0 + nrm(ks[7], (DEPTH, d), 0.1),
        'w_in': nrm(ks[8], (DEPTH, d, IN_WIDTH), d ** -0.5),
        'hg_lb_raw': 1.0 + nrm(ks[9], (DEPTH, HG_QK), 0.5),
        'hg_norm_g': 1.0 + nrm(ks[10], (DEPTH, HG_WIDTH), 0.1),
        'att_sink': nrm(ks[11], (DEPTH, ATT_HEADS), 0.5),
        'lru_conv_w': nrm(ks[12], (DEPTH, LRU_CONV, LRU_WIDTH), LRU_CONV ** -0.5),
        'lru_conv_b': nrm(ks[13], (DEPTH, LRU_WIDTH), 0.01),
        'lru_w_r': nrm(ks[14], (DEPTH, 2, LRU_BLOCKS, LRU_BLOCK, LRU_BLOCK), LRU_BLOCK ** -0.5),
        'lru_b_r': nrm(ks[15], (DEPTH, 2, LRU_WIDTH), 0.01),
        'lru_w_i': nrm(ks[16], (DEPTH, 2, LRU_BLOCKS, LRU_BLOCK, LRU_BLOCK), LRU_BLOCK ** -0.5),
        'lru_b_i': nrm(ks[17], (DEPTH, 2, LRU_WIDTH), 0.01),
        'lru_lambda': jnp.log(a_base) - jnp.log1p(-a_base),
        'w_out': nrm(ks[19], (DEPTH, MIX_WIDTH, d), MIX_WIDTH ** -0.5),
        'ffn_w_up': nrm(ks[20], (DEPTH, d, 2 * D_FF), d ** -0.5),
        'ffn_conv_w': nrm(ks[21], (DEPTH, FFN_CONV, 2 * D_FF), FFN_CONV ** -0.5),
        'ffn_conv_b': nrm(ks[22], (DEPTH, 2 * D_FF), 0.01),
        'ffn_w_down': nrm(ks[23], (DEPTH, D_FF, d), D_FF ** -0.5),
        'final_norm_g': 1.0 + nrm(ks[24], (d,), 0.1),
    }


def reference(x, c, ctx, c_ctx, ada_w, ada_b, norm_mix_g, norm_ffn_g, w_in, hg_lb_raw, hg_norm_g,
              att_sink, lru_conv_w, lru_conv_b, lru_w_r, lru_b_r, lru_w_i, lru_b_i, lru_lambda,
              w_out, ffn_w_up, ffn_conv_w, ffn_conv_b, ffn_w_down, final_norm_g):
    seq_len = x.shape[1]
    rows = seq_len // GRID_W
    cos, sin = rope_2d(rows)
    p = jax.nn.softmax(hg_lb_raw.astype(jnp.float32), axis=0)
    lbs = jnp.cumsum(p, axis=0) - p[0]
    sc = jax.nn.silu(c)
    scc = jax.nn.silu(c_ctx)
    h_ctx = ctx
    for layer in range(DEPTH):
        want_ctx = layer < DEPTH - 1
        mod = sc @ ada_w[layer] + ada_b[layer]
        mod_c = scc @ ada_w[layer] + ada_b[layer]
        sh1, sc1, g1, sh2, sc2, g2 = jnp.split(mod, 6, axis=-1)
        sh1c, sc1c, g1c, sh2c, sc2c, g2c = jnp.split(mod_c, 6, axis=-1)
        h_l = modulate(x, norm_mix_g[layer], sh1, sc1)
        h_c = modulate(h_ctx, norm_mix_g[layer], sh1c, sc1c)
        y_l, y_c = token_mixer(h_l, h_c, w_in[layer], w_out[layer], lbs[layer], hg_norm_g[layer],
                               att_sink[layer], lru_conv_w[layer], lru_conv_b[layer], lru_w_r[layer],
                               lru_b_r[layer], lru_w_i[layer], lru_b_i[layer], lru_lambda[layer],
                               cos, sin, want_ctx)
        x = x + g1[:, None, :] * y_l
        x = x + g2[:, None, :] * conv_ffn(modulate(x, norm_ffn_g[layer], sh2, sc2), ffn_w_up[layer],
                                          ffn_conv_w[layer], ffn_conv_b[layer], ffn_w_down[layer])
        if want_ctx:
            h_ctx = h_ctx + g1c * y_c
            h_ctx = h_ctx + g2c * conv_ffn(modulate(h_ctx, norm_ffn_g[layer], sh2c, sc2c), ffn_w_up[layer],
                                           ffn_conv_w[layer], ffn_conv_b[layer], ffn_w_down[layer])
    return rmsnorm(x, final_norm_g)
```

```python
import bisect
import contextlib
import numpy as np
import concourse.bass as bass
import concourse.mybir as mybir
from concourse.bass_utils import run_bass_kernel_spmd

F32 = mybir.dt.float32
BF16 = mybir.dt.bfloat16
AF = mybir.ActivationFunctionType
ALU = mybir.AluOpType

D = 1024
DEPTH = 4
NCTX = 256
NLAT = 4096
NT = NCTX + NLAT
NB = NT // 128
CH = 32
NCH = NT // CH
EPS = 1e-6
DFF = 2816
NC1 = 3200
NFA = 21

PV_G = {"cc": (0, 16), "fng": (16, 8), "lbraw": (24, 8)}
PV_GN = 32
PV_L = {}
_o = 0
for _n, _s in (("gmix", 8), ("gffn", 8), ("adab", 48), ("hgng", 2), ("sink", 8), ("lcw", 8), ("lcb", 2),
               ("lbr", 4), ("lbi", 4), ("lam", 4), ("fcw", 132), ("fcb", 44)):
    PV_L[_n] = (_o, _s)
    _o += _s
PV_LN = _o
NPV = PV_GN + DEPTH * PV_LN
DV_L = {}
_o = 0
for _n, _s in (("lb", 2), ("oml", 2), ("esink", 8), ("c1", 4), ("c2", 4)):
    DV_L[_n] = (_o, _s)
    _o += _s
DV_LN = _o
DV_GN = 16
NDV = DV_GN + DEPTH * DV_LN
CM = {"ident": 0, "blkones": 128, "hgm_f": 256, "hgm_b": 384, "mprev": 512, "mnext": 640, "ones": 768,
      "rm_f": 896, "rm_b": 1408, "chm": 1920}
NCM = 1924


ENGS = ("pe", "act", "dve", "pool", "sp")


class Slot:
    __slots__ = ("w", "r")

    def __init__(self):
        self.w = None
        self.r = []


class Buf:
    def __init__(self, h):
        self.h = h
        self.s = Slot()

    def __getitem__(self, k):
        return self.h[k]


class Op:
    __slots__ = ("eng", "emit", "deps", "needs_inc", "val", "is_dma", "sem", "ndma", "gidx", "barrier")

    def __init__(self, eng, emit, is_dma=False, ndma=1):
        self.eng = eng
        self.emit = emit
        self.deps = []
        self.needs_inc = False
        self.val = None
        self.is_dma = is_dma
        self.sem = None
        self.ndma = ndma
        self.gidx = 0
        self.barrier = False


class Prog:
    def __init__(self, nc, n_dma_sems=64):
        self.nc = nc
        self.all = []
        self.n_dma_sems = n_dma_sems
        self.dma_rr = 0
        self.dma_rr_sw = 0
        self.last = {}

    def op(self, eng, emit, reads=(), writes=(), is_dma=False, ndma=1):
        o = Op(eng, emit, is_dma, ndma)
        o.gidx = len(self.all)
        deps = set()
        for b in reads:
            s = b.s
            if s.w is not None:
                deps.add(s.w)
        for b in writes:
            s = b.s
            if s.w is not None:
                deps.add(s.w)
            deps.update(s.r)
        o.deps = list(deps)
        for d in o.deps:
            d.needs_inc = True
        for b in reads:
            b.s.r.append(o)
        for b in writes:
            b.s.w = o
            b.s.r = []
        if is_dma:
            if eng == "pool":
                o.sem = self.n_dma_sems - 16 + self.dma_rr_sw
                self.dma_rr_sw = (self.dma_rr_sw + 1) % 16
            else:
                o.sem = self.dma_rr
                self.dma_rr = (self.dma_rr + 1) % (self.n_dma_sems - 16)
            o.needs_inc = True
        else:
            self.last[eng] = o
        self.all.append(o)
        return o

    def dma(self, eng, out, in_, reads=(), writes=()):
        return self.op(eng, lambda e: e.dma_start(out=out, in_=in_), reads, writes, is_dma=True)

    def barrier(self):
        o = Op("all", None)
        o.barrier = True
        o.gidx = len(self.all)
        o.deps = list(self.last.values())
        for d in o.deps:
            d.needs_inc = True
        self.all.append(o)

    def build(self):
        nc = self.nc
        with contextlib.ExitStack() as st:
            esem = {e: st.enter_context(nc.semaphore("s_" + e)) for e in ("pe", "act", "dve", "pool")}
            dsem = [st.enter_context(nc.semaphore("d%d" % i)) for i in range(self.n_dma_sems)]
            cnt = {e: 0 for e in esem}
            dcnt = [0] * self.n_dma_sems
            dhist = [[] for _ in range(self.n_dma_sems)]
            bar_d = {}
            for o in self.all:
                if o.barrier:
                    bar_d[o.gidx] = list(dcnt)
                elif o.is_dma:
                    dcnt[o.sem] += 16 * o.ndma
                    o.val = dcnt[o.sem]
                    dhist[o.sem].append((o.gidx, o.val))
                elif o.needs_inc:
                    cnt[o.eng] += 1
                    o.val = cnt[o.eng]
            waited = {e: {} for e in ENGS}
            plans = {e: [] for e in ENGS}

            def addw(eng, ws):
                wl = []
                for key, v in ws.items():
                    if v <= 0 or waited[eng].get(key, 0) >= v:
                        continue
                    waited[eng][key] = v
                    wl.append((key, v))
                return wl

            for o in self.all:
                if o.barrier:
                    ws = {("e", d.eng): d.val for d in o.deps}
                    for i, v in enumerate(bar_d[o.gidx]):
                        ws[("d", i)] = v
                    for e in ENGS:
                        plans[e].append((None, addw(e, dict(ws))))
                    continue
                ws = {}
                if o.is_dma:
                    ws[("d", o.sem)] = o.val - 16 * o.ndma
                for d in o.deps:
                    if d.is_dma:
                        v = d.val
                        key = ("d", d.sem)
                    else:
                        v = d.val
                        key = ("e", d.eng)
                    if ws.get(key, 0) < v:
                        ws[key] = v
                plans[o.eng].append((o, addw(o.eng, ws)))
            fin = [(("d", i), dcnt[i]) for i in range(self.n_dma_sems) if dcnt[i] > 0]

            def semof(key):
                return dsem[key[1]] if key[0] == "d" else esem[key[1]]

            def run(engname, eng):
                for o, wl in plans[engname]:
                    for key, v in wl:
                        eng.wait_ge(semof(key), v)
                    if o is None:
                        continue
                    inst = o.emit(eng)
                    if o.is_dma:
                        inst.then_inc(dsem[o.sem], 16)
                    elif o.needs_inc:
                        if isinstance(inst, (list, tuple)):
                            inst = inst[-1]
                        inst.then_inc(esem[o.eng], 1)
                if engname == "sp":
                    for key, v in fin:
                        eng.wait_ge(semof(key), v)

            with nc.Block() as block:
                @block.tensor
                def _(e):
                    run("pe", e)

                @block.scalar
                def _(e):
                    run("act", e)

                @block.vector
                def _(e):
                    run("dve", e)

                @block.gpsimd
                def _(e):
                    run("pool", e)

                @block.sync
                def _(e):
                    run("sp", e)


class Arena:
    def __init__(self, nc, base, size):
        self.nc, self.base, self.size, self.top, self.n = nc, base, size, 0, 0

    def alloc(self, free, dt):
        nb = int(np.prod(free)) * (2 if dt == BF16 else 4)
        off = (self.top + 31) // 32 * 32
        assert off + nb <= self.size, ("SBUF arena overflow", off + nb, self.size)
        self.top = off + nb
        self.n += 1
        h = self.nc.alloc_sbuf_tensor_at("t%d" % self.n, [128] + list(free), dt, offset=self.base + off)
        return Buf(h)


class Ring:
    def __init__(self, bufs):
        self.b, self.i = bufs, 0

    def get(self):
        b = self.b[self.i]
        self.i = (self.i + 1) % len(self.b)
        return b


class _Stop(Exception):
    pass


def build_program(nlayers=DEPTH, debug=False, stop_after=None):
    nc = bass.Bass("TRN2", target_bir_lowering=False)
    dbgk = "ExternalOutput" if debug else "Internal"

    def din(name, shape, dt=F32):
        return nc.dram_tensor(name, list(shape), dt, kind="ExternalInput").ap()

    xT0 = din("xT0", [8, 128, NT])
    pv_d = din("pv", [128, NPV])
    cs_d = din("cs", [2, 128, NT])
    cm_d = din("cmask", [128, NCM])
    adaw = din("adaw", [DEPTH, 128, 8, 6 * D])
    w1_d = din("w1", [DEPTH, 128, 8, NC1])
    w2_d = din("w2", [DEPTH, 128, 8, D])
    w3_d = din("w3", [DEPTH, 128, 8, 2 * DFF])
    w4_d = din("w4", [DEPTH, 128, 22, D])
    wl_d = din("wlru", [DEPTH, 128, 8, 128])
    outT = nc.dram_tensor("outT", [8, 128, NLAT], F32, kind="ExternalOutput").ap()
    XA = nc.dram_tensor("XA", [8, 128, NT], F32, kind=dbgk).ap()
    XB = nc.dram_tensor("XB", [8, 128, NT], F32, kind=dbgk).ap()
    FA = nc.dram_tensor("FA", [NFA, 128, NT], BF16, kind=dbgk).ap()
    XL = nc.dram_tensor("XL", [2, 128, NT], F32, kind=dbgk).ap()
    VT = nc.dram_tensor("VT", [3, 128, NB, 128], BF16, kind=dbgk).ap()
    MX = nc.dram_tensor("MX", [8, 128, NT], BF16, kind=dbgk).ap()
    DVD = nc.dram_tensor("DVD", [128, NDV], F32, kind=dbgk).ap()
    DBG = nc.dram_tensor("DBG", [6, 128, 512], F32, kind=dbgk).ap()
    dXA, dXB, dFA, dXL, dVT, dMX = (Buf(None) for _ in range(6))

    base = (nc.sbuf_base + 31) // 32 * 32
    size = nc.sbuf_top - base - 64
    nc.alloc_sbuf_tensor("arena", [128, size // 4], F32)
    AR = Arena(nc, base, size)
    psall = nc.alloc_psum_tensor("psall", [128, 8 * 512], F32)
    banks = [Buf(psall[:, i * 512:(i + 1) * 512]) for i in range(8)]
    bring = Ring(banks)

    def span(i0, n):
        return psall[:, i0 * 512:(i0 + n) * 512]
    P = Prog(nc)

    def ACT(out, in_, func, reads, writes, scale=None, bias=None):
        kw = {}
        if scale is not None:
            kw["scale"] = scale
        if bias is not None:
            kw["bias"] = bias
        return P.op("act", lambda e: e.activation(out=out, in_=in_, func=func, **kw), reads, writes)

    def TT(eng, out, a, b, op, reads, writes):
        return P.op(eng, lambda e: e.tensor_tensor(out=out, in0=a, in1=b, op=op), reads, writes)

    def TS(eng, out, a, s1, s2, op0, op1, reads, writes):
        if op1 is None:
            return P.op(eng, lambda e: e.tensor_scalar(out=out, in0=a, scalar1=s1, scalar2=None, op0=op0), reads, writes)
        return P.op(eng, lambda e: e.tensor_scalar(out=out, in0=a, scalar1=s1, scalar2=s2, op0=op0, op1=op1), reads, writes)

    def STT(out, a, s, b, op0, op1, reads, writes):
        return P.op("dve", lambda e: e.scalar_tensor_tensor(out=out, in0=a, scalar=s, in1=b, op0=op0, op1=op1), reads, writes)

    def CP(eng, out, in_, reads, writes):
        return P.op(eng, lambda e: e.tensor_copy(out=out, in_=in_), reads, writes)

    def SCAN(out, d0, d1, initial, reads, writes):
        return P.op("dve", lambda e: e.tensor_tensor_scan(out=out, data0=d0, data1=d1, initial=initial,
                                                          op0=ALU.mult, op1=ALU.add), reads, writes)

    def MEMSET(eng, ap, val, writes):
        return P.op(eng, lambda e: e.memset(ap, val), [], writes)

    def MM(out, pairs, reads, writes):
        def emit(e):
            n = len(pairs)
            last = None
            for i, (l, r) in enumerate(pairs):
                last = e.matmul(out, lhsT=l, rhs=r, start=(i == 0), stop=(i == n - 1))
            return last
        return P.op("pe", emit, reads, writes)

    pv = AR.alloc([NPV], F32)
    dv = AR.alloc([NDV], F32)
    cmb = AR.alloc([NCM], BF16)
    dec = [AR.alloc([NCH], F32) for _ in range(4)]
    mv = [AR.alloc([128], F32) for _ in range(DEPTH)]
    persist_top = AR.top

    def pvl(l, name, sub=None):
        o, n = PV_L[name]
        o += PV_GN + l * PV_LN
        return pv.h[:, o:o + n]

    def dvl(l, name):
        o, n = DV_L[name]
        o += DV_GN + l * DV_LN
        return dv.h[:, o:o + n]

    def pvg(name):
        o, n = PV_G[name]
        return pv.h[:, o:o + n]

    def cmB(name, n=128):
        return cmb.h[:, CM[name]:CM[name] + n]


    P.op("pool", lambda e: e.memset(dv.h[:], 0.0), [], [dv])
    P.dma("sp", pv.h[:], pv_d, writes=[pv])
    P.dma("pool", cmb.h[:], cm_d, writes=[cmb])
    tA = AR.alloc([64], F32)
    tB = AR.alloc([64], F32)
    ACT(tA.h[:, 0:16], pvg("cc"), AF.Exp, [pv], [tA], scale=-1.0)
    ACT(tB.h[:, 0:16], tA.h[:, 0:16], AF.Ln, [tA], [tB], bias=1.0)
    ACT(tA.h[:, 0:16], tB.h[:, 0:16], AF.Exp, [tB], [tA], scale=-1.0)
    TT("dve", dv.h[:, 0:16], tA.h[:, 0:16], pvg("cc"), ALU.mult, [tA, pv], [dv])
    ACT(tA.h[:, 0:8], pvg("lbraw"), AF.Exp, [pv, tA], [tA])
    P.op("dve", lambda e: e.tensor_reduce(out=tB.h[:, 0:2], in_=tA.h[:, 0:8].rearrange("p (c l) -> p c l", l=4),
                                          axis=mybir.AxisListType.X, op=ALU.add), [tA], [tB])
    P.op("dve", lambda e: e.reciprocal(out=tB.h[:, 2:4], in_=tB.h[:, 0:2]), [tB], [tB])
    TT("dve", tA.h[:, 8:16].rearrange("p (c l) -> p c l", l=4), tA.h[:, 0:8].rearrange("p (c l) -> p c l", l=4),
       tB.h[:, 2:4].unsqueeze(2).to_broadcast([128, 2, 4]), ALU.mult, [tA, tB], [tA])
    pl = tA.h[:, 8:16].rearrange("p (c l) -> p c l", l=4)
    P.op("dve", lambda e: e.memset(dvl(0, "lb"), 0.0), [], [dv])
    CP("dve", dvl(1, "lb"), pl[:, :, 1], [tA], [dv])
    TT("dve", dvl(2, "lb"), dvl(1, "lb"), pl[:, :, 2], ALU.add, [tA, dv], [dv])
    TT("dve", dvl(3, "lb"), dvl(2, "lb"), pl[:, :, 3], ALU.add, [tA, dv], [dv])
    for l in range(DEPTH):
        TS("dve", dvl(l, "oml"), dvl(l, "lb"), -1.0, 1.0, ALU.mult, ALU.add, [dv], [dv])
        ACT(dvl(l, "esink"), pvl(l, "sink"), AF.Exp, [pv, dv], [dv])
        ACT(tB.h[:, 8:12], pvl(l, "lam"), AF.Exp, [pv, tB], [tB], scale=-1.0)
        ACT(tB.h[:, 12:16], tB.h[:, 8:12], AF.Ln, [tB], [tB], bias=1.0)
        TS("dve", dvl(l, "c1"), tB.h[:, 12:16], -8.0, None, ALU.mult, None, [tB, dv], [dv])
        TS("dve", dvl(l, "c2"), tB.h[:, 12:16], -16.0, None, ALU.mult, None, [tB, dv], [dv])
    def compute_mod_gen(l, wb2, pb):
        def load(mg):
            wb = wb2.get()
            P.dma("sp" if mg % 2 == 0 else "pool", wb.h[:], adaw[l, :, :, mg * 512:(mg + 1) * 512], writes=[wb])
            return wb
        wnext = load(0)
        for mg in range(12):
            wb = wnext
            if mg + 1 < 12:
                wnext = load(mg + 1)
            for mi in range(4):
                m = mg * 4 + mi
                MM(pb.h[:, 2 * m:2 * m + 2],
                   [(wb.h[:, k, mi * 128:(mi + 1) * 128], dv.h[:, 2 * k:2 * k + 2]) for k in range(8)],
                   [wb, dv], [pb])
            yield
        modv = mv[l].h[:, 0:96].rearrange("p (m j) -> p m j", j=2)
        TT("dve", modv, pb.h[:, 0:96].rearrange("p (m j) -> p m j", j=2),
           pvl(l, "adab").unsqueeze(2).to_broadcast([128, 48, 2]), ALU.add, [pb, pv], [mv[l]])
        for o_, c0, gn in ((96, 8, "gmix"), (112, 32, "gffn")):
            gv = mv[l].h[:, o_:o_ + 16].rearrange("p (k j) -> p k j", j=2)
            TS("dve", gv, modv[:, c0:c0 + 8, :], 1.0, None, ALU.add, None, [mv[l]], [mv[l]])
            TT("dve", gv, gv, pvl(l, gn).unsqueeze(2).to_broadcast([128, 8, 2]), ALU.mult, [mv[l], pv], [mv[l]])
        yield

    mark = AR.top
    for _ in compute_mod_gen(0, Ring([AR.alloc([8, 512], F32) for _ in range(2)]), bring.get()):
        pass
    if debug:
        P.dma("sp", DVD, dv.h[:], reads=[dv])
    P.barrier()
    AR.top = mark

    def modcol(l, chunk, j):
        o = 2 * chunk + j
        return mv[l].h[:, o:o + 1]

    def gscol(l, nm, k, j):
        o = (96 if nm == "gs1" else 112) + 2 * k + j
        return mv[l].h[:, o:o + 1]

    def rmsnorm_mod(l, xt_b, T, gsn, shc, j, sqb, lnb, rsb, hnr, hTb):
        ACT(sqb.h[:, :, :T], xt_b.h[:, :, :T], AF.Square, [xt_b], [sqb])
        pb = bring.get()
        MM(pb.h[:, :T], [(cmB("ones"), sqb.h[:, k, :T]) for k in range(8)], [sqb, cmb], [pb])
        ACT(lnb.h[:, :T], pb.h[:, :T], AF.Ln, [pb], [lnb], scale=1.0 / D, bias=EPS)
        ACT(rsb.h[:, :T], lnb.h[:, :T], AF.Exp, [lnb], [rsb], scale=-0.5)
        for k in range(8):
            hn = hnr.get()
            STT(hn.h[:, :T], xt_b.h[:, k, :T], gscol(l, gsn, k, j), rsb.h[:, :T], ALU.mult, ALU.mult,
                [xt_b, rsb, mv[l]], [hn])
            ACT(hTb.h[:, k, :T], hn.h[:, :T], AF.Identity, [hn, mv[l]], [hTb], bias=modcol(l, shc + k, j))

    Xin, Xmid = (XA, dXA), (XB, dXB)
    first_src = (xT0, Buf(None))

    def layer_body(l):
        src, dsrc = (first_src if l == 0 else Xin)
        AR.top = persist_top
        W1 = AR.alloc([8, NC1], BF16)
        for k in range(8):
            P.dma("pool", W1.h[:, k, :], w1_d[l, :, k, :], writes=[W1])
        xtr = Ring([AR.alloc([8, 512], F32) for _ in range(2)])
        csr = Ring([AR.alloc([2, 512], F32) for _ in range(2)])
        sqb = AR.alloc([8, 512], BF16)
        lnb = AR.alloc([512], F32)
        rsb = AR.alloc([512], F32)
        hnr = Ring([AR.alloc([512], F32) for _ in range(2)])
        hTb = AR.alloc([8, 512], BF16)
        tmr = Ring([AR.alloc([512], F32) for _ in range(6)])
        hgt = {X: [AR.alloc([512], F32) for _ in range(2 if X in ("q", "g") else 4)] for X in ("q", "g", "f0", "f1")}
        stg = AR.alloc([NFA, 512], BF16)
        xls = AR.alloc([2, 512], F32)
        vts = AR.alloc([3, 4, 128], BF16)
        sqq = AR.alloc([512], F32)
        tiles = [(0, 256)] + [(256 + 512 * i, 512) for i in range(8)]
        hTr = Ring([hTb, AR.alloc([8, 512], BF16)])

        def prepA(tile):
            n0, T = tile
            j = 1 if n0 < NCTX else 0
            xt_b = xtr.get()
            P.dma("sp", xt_b.h[:, :, :T], src.rearrange("c p n -> p c n")[:, :, n0:n0 + T], reads=[dsrc], writes=[xt_b])
            cs_b = csr.get()
            P.dma("sp", cs_b.h[:, :, :T], cs_d.rearrange("c p n -> p c n")[:, :, n0:n0 + T], writes=[cs_b])
            hT_ = hTr.get()
            rmsnorm_mod(l, xt_b, T, "gs1", 0, j, sqb, lnb, rsb, hnr, hT_)
            return cs_b, hT_

        nxtA = prepA(tiles[0])
        for ti_, (n0, T) in enumerate(tiles):
            cs_b, hTb = nxtA

            def proj(c):
                pb = bring.get()
                MM(pb.h[:, :T], [(W1.h[:, k, c * 128:(c + 1) * 128], hTb.h[:, k, :T]) for k in range(8)], [W1, hTb], [pb])
                return pb

            def silu_from(pb):
                t1, t2 = tmr.get(), tmr.get()
                ACT(t1.h[:, :T], pb.h[:, :T], AF.Exp, [pb], [t1], scale=-1.0)
                ACT(t2.h[:, :T], t1.h[:, :T], AF.Ln, [t1], [t2], bias=1.0)
                ACT(t1.h[:, :T], t2.h[:, :T], AF.Exp, [t2], [t1], scale=-1.0)
                return t1

            for hp in range(2):
                chains = ("q", "g", "f0", "f1")
                ccol = {"q": hp, "g": 6 + hp, "f0": 2 + hp, "f1": 4 + hp}
                pbk = {X: proj(ccol[X]) for X in chains}
                t1 = {X: hgt[X][0] for X in chains}
                t2 = {X: hgt[X][1] for X in chains}
                for X in chains:
                    ACT(t1[X].h[:, :T], pbk[X].h[:, :T], AF.Exp, [pbk[X]], [t1[X]], scale=-1.0)
                for X in chains:
                    ACT(t2[X].h[:, :T], t1[X].h[:, :T], AF.Ln, [t1[X]], [t2[X]], bias=1.0)
                for X in chains:
                    ACT(t1[X].h[:, :T], t2[X].h[:, :T], AF.Exp, [t2[X]], [t1[X]], scale=-1.0)
                sq_ = sqq
                TT("dve", sq_.h[:, :T], pbk["q"].h[:, :T], t1["q"].h[:, :T], ALU.mult, [pbk["q"], t1["q"]], [sq_])
                TT("dve", t2["g"].h[:, :T], pbk["g"].h[:, :T], t1["g"].h[:, :T], ALU.mult, [pbk["g"], t1["g"]], [t2["g"]])
                for dr in range(2):
                    X = "f%d" % dr
                    TS("dve", t2[X].h[:, :T], t1[X].h[:, :T], dvl(l, "oml")[:, hp:hp + 1], dvl(l, "lb")[:, hp:hp + 1],
                       ALU.mult, ALU.add, [t1[X], dv], [t2[X]])
                ACT(stg.h[:, 12 + hp, :T], t2["g"].h[:, :T], AF.Identity, [t2["g"], pv], [stg], scale=pvl(l, "hgng")[:, hp:hp + 1])
                for dr in range(2):
                    X = "f%d" % dr
                    ACT(t1[X].h[:, :T], t2[X].h[:, :T], AF.Ln, [t2[X]], [t1[X]])
                t3 = {}
                for dr in range(2):
                    X = "f%d" % dr
                    t3[X] = hgt[X][2]
                    if dr == 0:
                        SCAN(t3[X].h[:, :T], cmB("rm_f", 512)[:, :T], t1[X].h[:, :T], 0.0, [t1[X], cmb], [t3[X]])
                    else:
                        SCAN(t3[X].h[:, :T][:, ::-1], cmB("rm_b", 512)[:, :T][:, ::-1], t1[X].h[:, :T][:, ::-1], 0.0,
                             [t1[X], cmb], [t3[X]])
                t4 = {}
                for dr in range(2):
                    X = "f%d" % dr
                    t4[X] = hgt[X][3]
                    ACT(t1[X].h[:, :T], t3[X].h[:, :T], AF.Exp, [t3[X]], [t1[X]])
                    ACT(t4[X].h[:, :T], t3[X].h[:, :T], AF.Exp, [t3[X]], [t4[X]], scale=-1.0)
                    ACT(t2[X].h[:, :T], t2[X].h[:, :T], AF.Identity, [t2[X]], [t2[X]], scale=-1.0, bias=1.0)
                nchk = T // CH
                for dr in range(2):
                    X = "f%d" % dr
                    idx = hp * 2 + dr
                    TT("dve", stg.h[:, 4 + idx, :T], t2[X].h[:, :T], t4[X].h[:, :T], ALU.mult, [t2[X], t4[X]], [stg])
                    eend = t1[X].h[:, :T].rearrange("p (c j) -> p c j", j=CH)
                    eend = eend[:, :, CH - 1:CH] if dr == 0 else eend[:, :, 0:1]
                    TT("dve", stg.h[:, 8 + idx, :T].rearrange("p (c j) -> p c j", j=CH),
                       stg.h[:, 4 + idx, :T].rearrange("p (c j) -> p c j", j=CH), eend.to_broadcast([128, nchk, CH]), ALU.mult,
                       [t1[X], stg], [stg])
                    TT("dve", stg.h[:, idx, :T], sq_.h[:, :T], t1[X].h[:, :T], ALU.mult, [sq_, t1[X]], [stg])
                    ACT(dec[idx].h[:, n0 // CH:n0 // CH + nchk].unsqueeze(2), eend, AF.Copy, [t1[X]], [dec[idx]])
            if ti_ + 1 < len(tiles):
                nxtA = prepA(tiles[ti_ + 1])
            for ci, (cq, cr) in enumerate(((8, 12), (9, 13), (10, 14), (11, 15), (16, 17))):
                pq = proj(cq)
                pr = proj(cr)
                t1, t2 = tmr.get(), tmr.get()
                TT("dve", t1.h[:, :T], pq.h[:, :T], cs_b.h[:, 0, :T], ALU.mult, [pq, cs_b], [t1])
                TT("dve", t2.h[:, :T], pr.h[:, :T], cs_b.h[:, 1, :T], ALU.mult, [pr, cs_b], [t2])
                TT("dve", stg.h[:, 14 + ci, :T], t1.h[:, :T], t2.h[:, :T], ALU.add, [t1, t2], [stg])
            for jj in range(2):
                px = proj(18 + jj)
                ACT(xls.h[:, jj, :T], px.h[:, :T], AF.Identity, [px], [xls])
                py = proj(20 + jj)
                t1, t2, t3 = tmr.get(), tmr.get(), tmr.get()
                ACT(t1.h[:, :T], py.h[:, :T], AF.Square, [py], [t1])
                ACT(t2.h[:, :T], t1.h[:, :T], AF.Identity, [t1], [t2], scale=0.044715 * 1.5957691216, bias=1.5957691216)
                TT("dve", t3.h[:, :T], py.h[:, :T], t2.h[:, :T], ALU.mult, [py, t2], [t3])
                ACT(t1.h[:, :T], t3.h[:, :T], AF.Exp, [t3], [t1], scale=-1.0)
                ACT(t2.h[:, :T], t1.h[:, :T], AF.Ln, [t1], [t2], bias=1.0)
                ACT(t1.h[:, :T], t2.h[:, :T], AF.Exp, [t2], [t1], scale=-1.0)
                TT("dve", stg.h[:, 19 + jj, :T], py.h[:, :T], t1.h[:, :T], ALU.mult, [py, t1], [stg])
            for b in range(T // 128):
                pb = bring.get()
                MM(pb.h[:, 0:384], [(hTb.h[:, k, b * 128:(b + 1) * 128], W1.h[:, k, 2816:3200]) for k in range(8)],
                   [W1, hTb], [pb])
                ACT(vts.h[:, :, b, :], pb.h[:, 0:384].rearrange("p (g c) -> p g c", c=128), AF.Identity, [pb], [vts])
            nb_ = T // 128
            P.dma("pool", FA.rearrange("c p n -> p c n")[:, :, n0:n0 + T], stg.h[:, :, :T], reads=[stg], writes=[dFA])
            P.dma("pool", XL.rearrange("c p n -> p c n")[:, :, n0:n0 + T], xls.h[:, :, :T], reads=[xls], writes=[dXL])
            P.dma("pool", VT.rearrange("g p b c -> p g b c")[:, :, n0 // 128:n0 // 128 + nb_, :], vts.h[:, :, :nb_, :], reads=[vts], writes=[dVT])
        P.barrier()
        if stop_after == ("A", l):
            raise _Stop()
        AR.top = persist_top
        mask4 = AR.alloc([4, 128], BF16)
        for q4 in range(4):
            ACT(mask4.h[:, q4, :], cmB("hgm_f" if q4 % 2 == 0 else "hgm_b"), AF.Copy, [cmb], [mask4])
        markH = AR.top
        for hp in range(2):
            AR.top = markH
            vv = AR.alloc([NB, 128], BF16)
            P.dma("sp", vv.h[:], VT[hp], reads=[dVT], writes=[vv])
            Ul = AR.alloc([64, 128], F32)
            Uc = AR.alloc([64, 8], F32)
            drl = AR.alloc([64, 128], F32)
            drc = AR.alloc([64, 8], F32)
            Scf = AR.alloc([64, 8], F32)
            Scb = [AR.alloc([8, 128], BF16) for _ in range(2)]
            Slb = [AR.alloc([128, 128], BF16) for _ in range(2)]
            for bb in Scb + Slb:
                MEMSET("dve", bb.h[:], 0.0, [bb])
            markU = AR.top
            Slt = AR.alloc([64, 128], BF16)
            ko = AR.alloc([NT], BF16)
            kotr = Ring([AR.alloc([4, 4, 128], BF16) for _ in range(2)])
            for dr in range(2):
                idx = hp * 2 + dr
                P.dma("sp", ko.h[:], FA[8 + idx], reads=[dFA], writes=[ko])
                bring.b, bring.i = banks[6:8], 0
                pairs = Ring([(0, banks[0:2]), (2, banks[2:4]), (4, banks[4:6])])
                groups = [(0, 2)] + [(b0, 4) for b0 in range(2, NB, 4)]
                def stage_t(b0, nbk, ko=ko):
                    pt = bring.get()
                    ptb = pt.h.bitcast(BF16)

                    def emit_t(e, ptb=ptb, b0=b0, nbk=nbk, ko=ko):
                        last = None
                        for i in range(nbk):
                            last = e.transpose(ptb[:, i * 128:(i + 1) * 128], ko.h[:, (b0 + i) * 128:(b0 + i + 1) * 128], cmB("ident"))
                        return last
                    P.op("pe", emit_t, [ko, cmb], [pt])
                    kot = kotr.get()
                    TT("dve", kot.h[:, :nbk], ptb[:, 0:nbk * 128].rearrange("p (b k) -> p b k", k=128).unsqueeze(2).to_broadcast([128, nbk, 4, 128]),
                       cmB("chm", 4).unsqueeze(1).unsqueeze(3).to_broadcast([128, nbk, 4, 128]), ALU.mult, [pt, cmb], [kot])
                    return kot

                kot_next = stage_t(*groups[0])
                for gi, (b0, nbk) in enumerate(groups):
                    kot = kot_next
                    if gi + 1 < len(groups):
                        kot_next = stage_t(*groups[gi + 1])

                    pi0, pbufs = pairs.get()
                    pap = span(pi0, 2)

                    def emit_u(e, pap=pap, kot=kot, b0=b0, nbk=nbk, vv=vv):
                        last = None
                        for i in range(nbk):
                            for c in range(4):
                                for h in range(2):
                                    last = e.matmul(pap[h * 64:(h + 1) * 64, (i * 4 + c) * 64:(i * 4 + c + 1) * 64],
                                                    lhsT=kot.h[:, i, c, h * 64:(h + 1) * 64],
                                                    rhs=vv.h[:, b0 + i, h * 64:(h + 1) * 64], start=True, stop=True)
                        return last
                    P.op("pe", emit_u, [kot, vv], pbufs)
                    if b0 < 2:
                        dstb, c0 = Uc, 0
                    else:
                        dstb, c0 = Ul, 4 * (b0 - 2)
                    ncc = nbk * 4
                    for h in range(2):
                        srcv = pap[h * 64:(h + 1) * 64, 0:ncc * 64].rearrange("p (c e) -> p e c", e=64)
                        dstv = dstb.h[h * 64:(h + 1) * 64, :, c0:c0 + ncc]
                        if h == 0:
                            ACT(dstv, srcv, AF.Copy, pbufs, [dstb])
                        else:
                            CP("dve", dstv, srcv, pbufs, [dstb])
                bring.b, bring.i = banks, 0
                if stop_after == ("H1", l):
                    raise _Stop()
                dcv = dec[idx].h
                ACT(drc.h[:], dcv[:, 0:8].unsqueeze(1).to_broadcast([128, 64, 8]), AF.Copy, [dec[idx]], [drc])
                ACT(drl.h[:], dcv[:, 8:NCH].unsqueeze(1).to_broadcast([128, 64, 128]), AF.Copy, [dec[idx]], [drl])
                fc, fl = (0, 0) if dr == 0 else (7, 127)
                MEMSET("dve", drc.h[:, :, fc:fc + 1], 0.0, [drc])
                MEMSET("dve", drl.h[:, :, fl:fl + 1], 0.0, [drl])
                flat = lambda bf: bf.h[:].rearrange("p e c -> p (e c)")
                rv = (lambda ap: ap) if dr == 0 else (lambda ap: ap[:, ::-1])
                SCAN(rv(flat(Scf)), rv(flat(drc)), rv(flat(Uc)), 0.0, [drc, Uc], [Scf])
                for h in range(2):
                    ph = slice(h * 64, (h + 1) * 64)
                    ACT(Scb[dr].h[ph, :, h * 64:(h + 1) * 64], Scf.h[ph].rearrange("p e c -> p c e"), AF.Copy, [Scf], [Scb[dr]])
                lastc = 7 if dr == 0 else 0
                STT(Ul.h[:, :, fl], Scf.h[:, :, lastc], dcv[:, 8 + fl:9 + fl], Ul.h[:, :, fl], ALU.mult, ALU.add,
                    [Scf, dec[idx], Ul], [Ul])
                SCAN(rv(flat(Slt)), rv(flat(drl)), rv(flat(Ul)), 0.0, [drl, Ul], [Slt])
                for h in range(2):
                    ph = slice(h * 64, (h + 1) * 64)
                    if h == 0:
                        ACT(Slb[dr].h[ph, :, h * 64:(h + 1) * 64], Slt.h[ph].rearrange("p e c -> p c e"), AF.Copy, [Slt], [Slb[dr]])
                    else:
                        CP("dve", Slb[dr].h[ph, :, h * 64:(h + 1) * 64], Slt.h[ph].rearrange("p e c -> p c e"), [Slt], [Slb[dr]])
            if stop_after == ("H2", l):
                raise _Stop()
            P.barrier()
            AR.top = markU
            qkr = Ring([AR.alloc([2, 512], BF16) for _ in range(2)])
            kbr = Ring([AR.alloc([2, 2, 512], BF16) for _ in range(2)])
            for bb in kbr.b:
                MEMSET("dve", bb.h[:], 0.0, [bb])
            gsr = Ring([AR.alloc([512], BF16) for _ in range(2)])
            scr = Ring([AR.alloc([4, 4, 128], BF16) for _ in range(2)])
            osq = AR.alloc([512], BF16)
            lnh = AR.alloc([512], F32)
            rsh = AR.alloc([512], F32)
            th = AR.alloc([512], F32)
            yr = Ring([AR.alloc([512], BF16) for _ in range(2)])

            def state_prev(dr, cg):
                if dr == 0:
                    if cg == 0:
                        return None, None
                    if cg <= 8:
                        return Scb[0], Scb[0].h[:, cg - 1, :]
                    return Slb[0], Slb[0].h[:, cg - 9, :]
                if cg == 7:
                    return None, None
                if cg < 7:
                    return Scb[1], Scb[1].h[:, cg + 1, :]
                if cg == NCH - 1:
                    return Scb[1], Scb[1].h[:, 0, :]
                return Slb[1], Slb[1].h[:, cg - 8 + 1, :]

            def partA(n0, T):
                qk = qkr.get()
                kbp = kbr.get()
                for dr in range(2):
                    P.dma("sp", qk.h[:, dr, :T], FA[hp * 2 + dr][:, n0:n0 + T], reads=[dFA], writes=[qk])
                    for h in range(2):
                        ph = slice(h * 64, (h + 1) * 64)
                        P.dma("act", kbp.h[ph, h, dr, :T], FA[4 + hp * 2 + dr][ph, n0:n0 + T], reads=[dFA], writes=[kbp])
                gsb = gsr.get()
                P.dma("sp", gsb.h[:, :T], FA[12 + hp][:, n0:n0 + T], reads=[dFA], writes=[gsb])
                po = bring.get()
                nbk = T // 128
                qb_ = banks[0:4]
                qap = span(0, nbk)

                def emit_s(e, qap=qap, qk=qk, kbp=kbp, nbk=nbk):
                    last = None
                    for bi in range(nbk):
                        cs_ = slice(bi * 128, (bi + 1) * 128)
                        for h in range(2):
                            for dr in range(2):
                                o0 = bi * 512 + (h * 2 + dr) * 128
                                last = e.matmul(qap[:, o0:o0 + 128], lhsT=kbp.h[:, h, dr, cs_], rhs=qk.h[:, dr, cs_], start=True, stop=True)
                    return last
                P.op("pe", emit_s, [qk, kbp], qb_[:nbk])
                sc = scr.get()
                TT("dve", sc.h[:, :nbk], qap.rearrange("p (b a k) -> p b a k", a=4, k=128),
                   mask4.h[:].unsqueeze(1).to_broadcast([128, nbk, 4, 128]), ALU.mult, qb_[:nbk] + [mask4], [sc])
                seq = []
                for bi in range(nbk):
                    b = n0 // 128 + bi
                    cs_ = slice(bi * 128, (bi + 1) * 128)
                    for dr in range(2):
                        for h in range(2):
                            ph = slice(h * 64, (h + 1) * 64)
                            seq.append((po.h[ph, cs_], vv.h[:, b, h * 64:(h + 1) * 64], sc.h[:, bi, h * 2 + dr, :], dr == 0, False))
                        for c in range(4):
                            _, sap = state_prev(dr, 4 * b + c)
                            if sap is None:
                                continue
                            c0 = bi * 128 + c * 32
                            seq.append((po.h[:, c0:c0 + 32], sap, qk.h[:, dr, c0:c0 + 32], False, False))
                    seq[-1] = seq[-1][:4] + (True,)

                return dict(n0=n0, T=T, po=po, seq=seq, sc=sc, qk=qk, gsb=gsb)

            def partB1(cx):
                seq, sc, qk, po = cx['seq'], cx['sc'], cx['qk'], cx['po']
                def emit_o(e, seq=seq):
                    last = None
                    for (o_, l_, r_, st_, sp_) in seq:
                        last = e.matmul(o_, lhsT=l_, rhs=r_, start=st_, stop=sp_)
                    return last
                P.op("pe", emit_o, [sc, vv, qk, Scb[0], Scb[1], Slb[0], Slb[1]], [po])

            def partB2(cx):
                n0, T, po, gsb = cx['n0'], cx['T'], cx['po'], cx['gsb']
                ACT(osq.h[:, :T], po.h[:, :T], AF.Square, [po], [osq])
                pss = bring.get()
                MM(pss.h[:, :T], [(cmB("blkones"), osq.h[:, :T])], [osq, cmb], [pss])
                ACT(lnh.h[:, :T], pss.h[:, :T], AF.Ln, [pss], [lnh], scale=1.0 / 64, bias=EPS)
                ACT(rsh.h[:, :T], lnh.h[:, :T], AF.Exp, [lnh], [rsh], scale=-0.5)
                TT("dve", th.h[:, :T], po.h[:, :T], rsh.h[:, :T], ALU.mult, [po, rsh], [th])
                yb = yr.get()
                TT("dve", yb.h[:, :T], th.h[:, :T], gsb.h[:, :T], ALU.mult, [th, gsb], [yb])
                P.dma("pool", MX[hp][:, n0:n0 + T], yb.h[:, :T], reads=[yb], writes=[dMX])

            bring.b, bring.i = banks[4:8], 0
            cx = partA(*tiles[0])
            for ti_ in range(len(tiles)):
                partB1(cx)
                nx = partA(*tiles[ti_ + 1]) if ti_ + 1 < len(tiles) else None
                partB2(cx)
                cx = nx
            bring.b, bring.i = banks, 0
            P.barrier()
        if stop_after == ("H", l):
            raise _Stop()
        AR.top = persist_top
        wlb = AR.alloc([8, 128], BF16)
        P.dma("pool", wlb.h[:], wl_d[l], writes=[wlb])
        negb = AR.alloc([8], F32)
        TS("dve", negb.h[:, 0:4], pvl(l, "lbr"), -1.0, None, ALU.mult, None, [pv], [negb])
        TS("dve", negb.h[:, 4:8], pvl(l, "lbi"), -1.0, None, ALU.mult, None, [pv], [negb])
        xp = AR.alloc([NT + 8], F32)
        xc = AR.alloc([NT], F32)
        xcb = AR.alloc([NT], BF16)
        gyb = AR.alloc([NT], BF16)
        hs = AR.alloc([NT], F32)
        hb = AR.alloc([NT], F32)
        a_ = AR.alloc([NT], F32)
        u_ = AR.alloc([NT], F32)
        ym = AR.alloc([NT], BF16)
        tl = Ring([AR.alloc([512], F32) for _ in range(8)])
        modgen = None
        if l + 1 < nlayers:
            bring.b, bring.i = banks[0:7], 0
            modgen = compute_mod_gen(l + 1, Ring([AR.alloc([8, 512], F32) for _ in range(2)]), banks[7])
        lstep = [0]
        for jj in range(2):
            MEMSET("dve", xp.h[:], 0.0, [xp])
            P.dma("sp", xp.h[:, 2:2 + NCTX], XL[jj][:, 0:NCTX], reads=[dXL], writes=[xp])
            P.dma("sp", xp.h[:, 261:261 + NLAT], XL[jj][:, NCTX:NT], reads=[dXL], writes=[xp])
            P.dma("sp", gyb.h[:], FA[19 + jj], reads=[dFA], writes=[gyb])
            lcw = pvl(l, "lcw")
            lcb = pvl(l, "lcb")
            for (off, n, d0) in ((0, NCTX, 0), (259, NLAT, NCTX)):
                for (s0, sn) in [(i, min(1024, n - i)) for i in range(0, n, 1024)]:
                    dst = xc.h[:, d0 + s0:d0 + s0 + sn]
                    TS("dve", dst, xp.h[:, off + s0:off + s0 + sn], lcw[:, jj * 4:jj * 4 + 1], lcb[:, jj:jj + 1], ALU.mult, ALU.add,
                       [xp, pv], [xc])
                    for k in range(1, 4):
                        STT(dst, xp.h[:, off + s0 + k:off + s0 + k + sn], lcw[:, jj * 4 + k:jj * 4 + k + 1], dst, ALU.mult, ALU.add,
                            [xp, pv, xc], [xc])
            ACT(xcb.h[:], xc.h[:], AF.Copy, [xc], [xcb])
            for dr in range(2):
                for (n0, T) in tiles:
                    pr, pi = bring.get(), bring.get()
                    MM(pr.h[:, :T], [(wlb.h[:, dr * 4 + 0 + jj, :], xcb.h[:, n0:n0 + T])], [wlb, xcb], [pr])
                    MM(pi.h[:, :T], [(wlb.h[:, dr * 4 + 2 + jj, :], xcb.h[:, n0:n0 + T])], [wlb, xcb], [pi])

                    def sig(pb, bcol):
                        t1, t2 = tl.get(), tl.get()
                        ACT(t1.h[:, :T], pb.h[:, :T], AF.Exp, [pb, negb], [t1], scale=-1.0, bias=negb.h[:, bcol:bcol + 1])
                        ACT(t2.h[:, :T], t1.h[:, :T], AF.Ln, [t1], [t2], bias=1.0)
                        ACT(t1.h[:, :T], t2.h[:, :T], AF.Exp, [t2], [t1], scale=-1.0)
                        return t1
                    r = sig(pr, dr * 2 + jj)
                    ig = sig(pi, 4 + dr * 2 + jj)
                    ACT(a_.h[:, n0:n0 + T], r.h[:, :T], AF.Exp, [r, dv], [a_], scale=dvl(l, "c1")[:, dr * 2 + jj:dr * 2 + jj + 1])
                    m1, m2 = tl.get(), tl.get()
                    ACT(m1.h[:, :T], r.h[:, :T], AF.Exp, [r, dv], [m1], scale=dvl(l, "c2")[:, dr * 2 + jj:dr * 2 + jj + 1])
                    ACT(m2.h[:, :T], m1.h[:, :T], AF.Ln, [m1], [m2], scale=-1.0, bias=1.0)
                    ACT(m1.h[:, :T], m2.h[:, :T], AF.Exp, [m2], [m1], scale=0.5)
                    TT("dve", m2.h[:, :T], ig.h[:, :T], xc.h[:, n0:n0 + T], ALU.mult, [ig, xc, m2], [m2])
                    TT("dve", u_.h[:, n0:n0 + T], m2.h[:, :T], m1.h[:, :T], ALU.mult, [m1, m2], [u_])
                    lstep[0] += 1
                    if modgen is not None and lstep[0] % 2 == 0:
                        next(modgen, None)
                if dr == 0:
                    P.op("dve", lambda e: e.tensor_tensor_scan(out=hs.h[:], data0=a_.h[:], data1=u_.h[:], initial=0.0,
                                                                op0=ALU.mult, op1=ALU.add), [a_, u_], [hs])
                else:
                    P.op("dve", lambda e: e.tensor_tensor_scan(out=hb.h[:, 0:NCTX][:, ::-1], data0=a_.h[:, 0:NCTX][:, ::-1],
                                                                data1=u_.h[:, 0:NCTX][:, ::-1], initial=0.0,
                                                                op0=ALU.mult, op1=ALU.add), [a_, u_], [hb])
                    P.op("dve", lambda e: e.tensor_tensor_scan(out=hb.h[:, NCTX:NT][:, ::-1], data0=a_.h[:, NCTX:NT][:, ::-1],
                                                                data1=u_.h[:, NCTX:NT][:, ::-1], initial=hb.h[:, 0:1],
                                                                op0=ALU.mult, op1=ALU.add), [a_, u_, hb], [hb])
                    TT("dve", hs.h[:], hs.h[:], hb.h[:], ALU.add, [hs, hb], [hs])
            TT("dve", ym.h[:], hs.h[:], gyb.h[:], ALU.mult, [hs, gyb], [ym])
            P.dma("sp", MX[6 + jj], ym.h[:], reads=[ym], writes=[dMX])
        if modgen is not None:
            for _ in modgen:
                pass
        bring.b, bring.i = banks, 0
        P.barrier()
        if stop_after == ("L", l):
            raise _Stop()
        AR.top = persist_top
        W3 = AR.alloc([8, 2 * DFF], BF16)
        W4 = AR.alloc([22, D], BF16)
        for k in range(8):
            P.dma("pool", W3.h[:, k, :], w3_d[l, :, k, :], writes=[W3])
        for k in range(0, 22, 2):
            P.dma("pool", W4.h[:, k:k + 2, :], w4_d[l, :, k:k + 2, :], writes=[W4])
        markW = AR.top
        krp = [AR.alloc([NT], BF16) for _ in range(2)]
        for kvh in range(2):
            MEMSET("dve", krp[kvh].h[:], 0.0, [krp[kvh]])
            P.dma("sp", krp[kvh].h[kvh * 64:(kvh + 1) * 64, :], FA[18][kvh * 64:(kvh + 1) * 64, :], reads=[dFA], writes=[krp[kvh]])
        va = AR.alloc([NB, 128], BF16)
        P.dma("sp", va.h[:], VT[2], reads=[dVT], writes=[va])
        qr_ = Ring([AR.alloc([4, 128], BF16) for _ in range(3)])
        er = Ring([AR.alloc([4, 128], BF16) for _ in range(12)])
        denr = Ring([AR.alloc([4, 128], F32) for _ in range(2)])
        recr = Ring([AR.alloc([4, 128], F32) for _ in range(2)])
        ysr = Ring([AR.alloc([4, 128], BF16) for _ in range(2)])
        mask2 = cmb.h[:, CM["mprev"]:CM["mprev"] + 256].rearrange("p (a b) -> p a b", b=128)
        pend = None
        for qb in range(NB):
            if qb < 2:
                keys = [(0, None), (1, None)]
            else:
                keys = [(0, None), (1, None)]
                if qb > 2:
                    keys.append((qb - 1, 0))
                keys.append((qb, None))
                if qb < NB - 1:
                    keys.append((qb + 1, 1))
            qt = qr_.get()
            P.dma("sp", qt.h[:], FA[14:18].rearrange("c p n -> p c n")[:, :, qb * 128:(qb + 1) * 128], reads=[dFA], writes=[qt])
            ys = ysr.get()
            for kvh in range(2):
                ph = slice(kvh * 64, (kvh + 1) * 64)
                ebs = []
                for (kb_, mk) in keys:
                    ps = bring.get()
                    MM(ps.h[:], [(krp[kvh].h[:, kb_ * 128:(kb_ + 1) * 128], qt.h[:, :, :])], [krp[kvh], qt], [ps])
                    eb = er.get()
                    ACT(eb.h[:], ps.h[:].rearrange("p (a b) -> p a b", b=128), AF.Exp, [ps], [eb], scale=0.125)
                    if mk is not None:
                        TT("dve", eb.h[:], eb.h[:], mask2[:, mk:mk + 1, :].to_broadcast([128, 4, 128]), ALU.mult, [eb, cmb], [eb])
                    ebs.append((kb_, eb))

                def finish(ebs=ebs, kvh=kvh, ph=ph, ys=ys, qb=qb, lastk=(kvh == 1)):
                    po, pd = bring.get(), bring.get()
                    MM(po.h[:], [(va.h[:, kb_, :], eb.h[:].rearrange("p a b -> p (a b)")) for kb_, eb in ebs], [va] + [eb for _, eb in ebs], [po])
                    MM(pd.h[:], [(cmB("ones"), eb.h[:].rearrange("p a b -> p (a b)")) for kb_, eb in ebs], [cmb] + [eb for _, eb in ebs], [pd])
                    den, rec = denr.get(), recr.get()
                    TT("dve", den.h[ph], pd.h[ph, :].rearrange("p (a b) -> p a b", b=128),
                       dvl(l, "esink")[ph, kvh * 4:(kvh + 1) * 4].unsqueeze(2).to_broadcast([64, 4, 128]), ALU.add, [pd, dv], [den])
                    ACT(den.h[ph], den.h[ph], AF.Ln, [den], [den])
                    ACT(rec.h[ph], den.h[ph], AF.Exp, [den], [rec], scale=-1.0)
                    TT("dve", ys.h[ph], po.h[ph, :].rearrange("p (a b) -> p a b", b=128), rec.h[ph], ALU.mult, [po, rec], [ys])
                    if lastk:
                        P.dma("sp", MX[2:6].rearrange("c p n -> p c n")[:, :, qb * 128:(qb + 1) * 128], ys.h[:], reads=[ys], writes=[dMX])
                if pend is not None:
                    pend()
                pend = finish
        pend()
        P.barrier()
        if stop_after == ("T", l):
            raise _Stop()
        AR.top = markW
        W2 = AR.alloc([8, D], BF16)
        P.dma("pool", W2.h[:], w2_d[l], writes=[W2])
        mxr = Ring([AR.alloc([8, 256], BF16) for _ in range(2)])
        xir = Ring([AR.alloc([8, 256], F32) for _ in range(2)])
        xor_ = Ring([AR.alloc([8, 256], F32) for _ in range(2)])
        for (n0, T) in [(i, 256) for i in range(0, NT, 256)]:
            j = 1 if n0 < NCTX else 0
            mt, xi, xo = mxr.get(), xir.get(), xor_.get()
            P.dma("sp", mt.h[:, :, :T], MX.rearrange("c p n -> p c n")[:, :, n0:n0 + T], reads=[dMX], writes=[mt])
            P.dma("act", xi.h[:, :, :T], src.rearrange("c p n -> p c n")[:, :, n0:n0 + T], reads=[dsrc], writes=[xi])
            for oc in range(8):
                pb = bring.get()
                MM(pb.h[:, :T], [(W2.h[:, k, oc * 128:(oc + 1) * 128], mt.h[:, k, :T]) for k in range(8)], [W2, mt], [pb])
                STT(xo.h[:, oc, :T], pb.h[:, :T], modcol(l, 16 + oc, j), xi.h[:, oc, :T], ALU.mult, ALU.add, [pb, xi, mv[l]], [xo])
            P.dma("pool", Xmid[0].rearrange("c p n -> p c n")[:, :, n0:n0 + T], xo.h[:, :, :T], reads=[xo], writes=[Xmid[1]])
        P.barrier()
        if stop_after == ("F1", l):
            raise _Stop()
        AR.top = markW
        LF = 256
        xwr = Ring([AR.alloc([8, LF + 2], F32) for _ in range(2)])
        sq2 = AR.alloc([8, LF + 2], BF16)
        ln2 = AR.alloc([LF + 2], F32)
        rs2 = AR.alloc([LF + 2], F32)
        hn2 = Ring([AR.alloc([LF + 2], F32) for _ in range(2)])
        h2 = AR.alloc([8, LF + 2], BF16)
        actb = AR.alloc([22, LF], BF16)
        agr = Ring([AR.alloc([LF], F32) for _ in range(2)])
        avr = Ring([AR.alloc([LF], F32) for _ in range(2)])
        sgr = Ring([AR.alloc([LF], F32) for _ in range(2)])
        xo2 = AR.alloc([8, LF], F32)
        fcw = pvl(l, "fcw")
        fcb = pvl(l, "fcb")
        ftiles = [(0, NCTX, s) for s in range(0, NCTX, LF)] + [(NCTX, NT, s) for s in range(NCTX, NT, LF)]
        h2r = Ring([h2, AR.alloc([8, LF + 2], BF16)])

        def prep(ft):
            g0, g1, s = ft
            j = 1 if s < NCTX else 0
            L = min(LF, g1 - s)
            wlo, whi = max(s - 1, g0), min(s + L + 1, g1)
            W = whi - wlo
            co = s - wlo
            xw = xwr.get()
            h2_ = h2r.get()
            P.dma("sp", xw.h[:, :, :W], Xmid[0].rearrange("c p n -> p c n")[:, :, wlo:whi], reads=[Xmid[1]], writes=[xw])
            rmsnorm_mod(l, xw, W, "gs2", 24, j, sq2, ln2, rs2, hn2, h2_)
            return (s, j, L, W, co, xw, h2_)

        nxt = prep(ftiles[0])
        for ti in range(len(ftiles)):
            s, j, L, W, co, xw, h2_ = nxt
            i0 = 1 - co
            i1 = min(L, W - 1 - co)
            for m in range(22):
                accs = []
                for (cidx, ring_) in ((m, agr), (22 + m, avr)):
                    pb = bring.get()
                    MM(pb.h[:, :W], [(W3.h[:, k, cidx * 128:(cidx + 1) * 128], h2_.h[:, k, :W]) for k in range(8)], [W3, h2_], [pb])
                    acc = ring_.get()
                    ACT(acc.h[:, :L], pb.h[:, co:co + L], AF.Identity, [pb, pv], [acc],
                        scale=fcw[:, cidx * 3 + 1:cidx * 3 + 2], bias=fcb[:, cidx:cidx + 1])
                    STT(acc.h[:, i0:L], pb.h[:, co + i0 - 1:co + L - 1], fcw[:, cidx * 3:cidx * 3 + 1], acc.h[:, i0:L],
                        ALU.mult, ALU.add, [pb, pv, acc], [acc])
                    STT(acc.h[:, 0:i1], pb.h[:, co + 1:co + 1 + i1], fcw[:, cidx * 3 + 2:cidx * 3 + 3], acc.h[:, 0:i1],
                        ALU.mult, ALU.add, [pb, pv, acc], [acc])
                    accs.append(acc)
                sgt = sgr.get()
                ACT(sgt.h[:, :L], accs[0].h[:, :L], AF.Silu, [accs[0]], [sgt])
                TT("dve", actb.h[:, m, :L], sgt.h[:, :L], accs[1].h[:, :L], ALU.mult, [sgt, accs[1]], [actb])
            if ti + 1 < len(ftiles):
                nxt = prep(ftiles[ti + 1])
            for oc in range(8):
                pb = bring.get()
                MM(pb.h[:, :L], [(W4.h[:, m, oc * 128:(oc + 1) * 128], actb.h[:, m, :L]) for m in range(22)], [W4, actb], [pb])
                STT(xo2.h[:, oc, :L], pb.h[:, :L], modcol(l, 40 + oc, j), xw.h[:, oc, co:co + L], ALU.mult, ALU.add, [pb, xw, mv[l]], [xo2])
            P.dma("pool", Xin[0].rearrange("c p n -> p c n")[:, :, s:s + L], xo2.h[:, :, :L], reads=[xo2], writes=[Xin[1]])
        P.barrier()
        if stop_after == ("F2", l):
            raise _Stop()

    def final_norm():
        AR.top = persist_top
        xzr = Ring([AR.alloc([8, 512], F32) for _ in range(2)])
        sqz = AR.alloc([8, 512], BF16)
        lnz = AR.alloc([512], F32)
        rsz = AR.alloc([512], F32)
        ozr = Ring([AR.alloc([8, 512], F32) for _ in range(2)])
        for i in range(8):
            n0 = NCTX + 512 * i
            xz, oz = xzr.get(), ozr.get()
            P.dma("sp", xz.h[:], Xin[0].rearrange("c p n -> p c n")[:, :, n0:n0 + 512], reads=[Xin[1]], writes=[xz])
            ACT(sqz.h[:], xz.h[:], AF.Square, [xz], [sqz])
            pb = bring.get()
            MM(pb.h[:], [(cmB("ones"), sqz.h[:, k, :]) for k in range(8)], [sqz, cmb], [pb])
            ACT(lnz.h[:], pb.h[:], AF.Ln, [pb], [lnz], scale=1.0 / D, bias=EPS)
            ACT(rsz.h[:], lnz.h[:], AF.Exp, [lnz], [rsz], scale=-0.5)
            for k in range(8):
                STT(oz.h[:, k, :], xz.h[:, k, :], pvg("fng")[:, k:k + 1], rsz.h[:], ALU.mult, ALU.mult, [xz, rsz, pv], [oz])
            P.dma("sp", outT[:, :, 512 * i:512 * (i + 1)].rearrange("c p n -> p c n"), oz.h[:], reads=[oz])
    try:
        for l in range(nlayers):
            layer_body(l)
        if nlayers == DEPTH:
            final_norm()
    except _Stop:
        pass
    P.build()
    return nc


def _fm(v, nchunk):
    return np.ascontiguousarray(np.asarray(v, np.float32).reshape(nchunk, 128).T)


def host_consts():
    cm = np.zeros((128, NCM), np.float32)
    i = np.arange(128)
    cm[:, CM["ident"]:CM["ident"] + 128] = np.eye(128)
    cm[:, CM["blkones"]:CM["blkones"] + 128] = (i[:, None] // 64 == i[None, :] // 64)
    same = (i[:, None] // CH == i[None, :] // CH)
    cm[:, CM["hgm_f"]:CM["hgm_f"] + 128] = same & (i[:, None] <= i[None, :])
    cm[:, CM["hgm_b"]:CM["hgm_b"] + 128] = same & (i[:, None] >= i[None, :])
    cm[:, CM["mprev"]:CM["mprev"] + 128] = (i[:, None] >= i[None, :])
    cm[:, CM["mnext"]:CM["mnext"] + 128] = (i[:, None] <= i[None, :])
    cm[:, CM["ones"]:CM["ones"] + 128] = 1.0
    t = np.arange(512)
    cm[:, CM["rm_f"]:CM["rm_f"] + 512] = (t % CH != 0)[None, :]
    cm[:, CM["rm_b"]:CM["rm_b"] + 512] = (t % CH != CH - 1)[None, :]
    cm[:, CM["chm"]:CM["chm"] + 4] = (i[:, None] // CH == np.arange(4)[None, :])
    nfreq = 16
    inv = (10000.0 ** (-np.arange(nfreq, dtype=np.float32) / nfreq)).astype(np.float32)
    pos = np.arange(NLAT)
    r = (pos // 64).astype(np.float32)
    c = (pos % 64).astype(np.float32)
    ang = np.concatenate([r[:, None] * inv, c[:, None] * inv], axis=-1).astype(np.float32)
    cos, sin = np.cos(ang).astype(np.float32), np.sin(ang).astype(np.float32)
    cs = np.zeros((2, 128, NT), np.float32)
    cs[0, :, :NCTX] = 1.0
    p = np.arange(128)
    cs[0, :, NCTX:] = cos.T[p % 32, :]
    sgn = np.where((p % 64) < 32, -1.0, 1.0).astype(np.float32)
    cs[1, :, NCTX:] = sin.T[p % 32, :] * sgn[:, None]
    return cm, cs


def w1_cols():
    cols = []
    cols += list(range(0, 256)) + list(range(256, 512)) + list(range(512, 768)) + list(range(1024, 1280))
    aq = 1280
    for jq in range(4):
        for h in (jq, 4 + jq):
            cols += list(range(aq + h * 64, aq + h * 64 + 64))
    for jq in range(4):
        for h in (jq, 4 + jq):
            cols += list(range(aq + h * 64 + 32, aq + h * 64 + 64)) + list(range(aq + h * 64, aq + h * 64 + 32))
    ak = 1792
    cols += list(range(ak, ak + 128))
    for h in range(2):
        cols += list(range(ak + h * 64 + 32, ak + h * 64 + 64)) + list(range(ak + h * 64, ak + h * 64 + 32))
    cols += list(range(2048, 2304)) + list(range(2304, 2560))
    cols += list(range(768, 1024)) + list(range(1920, 2048))
    assert len(cols) == NC1
    return np.array(cols)


def w2_rows():
    rows = list(range(0, 256))
    for g in range(4):
        for h in (g, 4 + g):
            rows += list(range(256 + h * 64, 256 + h * 64 + 64))
    rows += list(range(768, 1024))
    return np.array(rows)


def kmajor(w, nk):
    return np.ascontiguousarray(w.reshape(nk, 128, w.shape[-1]).transpose(1, 0, 2))


def prep_shared(inp):
    f = lambda k: np.asarray(inp[k], np.float32)
    cm, cs = host_consts()
    sh = {"cmask": cm, "cs": cs}
    sh["adaw"] = np.stack([kmajor(f("ada_w")[l], 8) for l in range(DEPTH)])
    c1 = w1_cols()
    sh["w1"] = np.stack([kmajor(f("w_in")[l][:, c1], 8) for l in range(DEPTH)])
    r2 = w2_rows()
    sh["w2"] = np.stack([kmajor(f("w_out")[l][r2, :], 8) for l in range(DEPTH)])
    sh["w3"] = np.stack([kmajor(f("ffn_w_up")[l], 8) for l in range(DEPTH)])
    sh["w4"] = np.stack([kmajor(f("ffn_w_down")[l], 22) for l in range(DEPTH)])
    wl = np.zeros((DEPTH, 128, 8, 128), np.float32)
    for l in range(DEPTH):
        for d in range(2):
            for gi, nm in enumerate(("lru_w_r", "lru_w_i")):
                w = f(nm)[l, d]
                for ch in range(2):
                    for hb in range(2):
                        wl[l, hb * 64:(hb + 1) * 64, d * 4 + gi * 2 + ch, hb * 64:(hb + 1) * 64] = w[ch * 2 + hb]
    sh["wlru"] = wl
    return sh


def prep_pv(inp, b):
    f = lambda k: np.asarray(inp[k], np.float32)
    pv = np.zeros((128, NPV), np.float32)
    cc = np.stack([_fm(f("c")[b], 8), _fm(f("c_ctx"), 8)], axis=-1)
    pv[:, 0:16] = cc.reshape(128, 16)
    pv[:, 16:24] = _fm(f("final_norm_g"), 8)
    lbr = np.stack([_fm(f("hg_lb_raw")[l], 2) for l in range(DEPTH)], axis=-1)
    pv[:, 24:32] = lbr.reshape(128, 8)
    for l in range(DEPTH):
        def put(name, arr):
            o, n = PV_L[name]
            o += PV_GN + l * PV_LN
            pv[:, o:o + n] = arr.reshape(128, n)
        put("gmix", _fm(f("norm_mix_g")[l], 8))
        put("gffn", _fm(f("norm_ffn_g")[l], 8))
        put("adab", _fm(f("ada_b")[l], 48))
        put("hgng", _fm(f("hg_norm_g")[l], 2))
        put("sink", np.broadcast_to(f("att_sink")[l][None, :], (128, 8)))
        put("lcw", np.stack([_fm(f("lru_conv_w")[l][t], 2) for t in range(4)], axis=-1))
        put("lcb", _fm(f("lru_conv_b")[l], 2))
        put("lbr", np.stack([_fm(f("lru_b_r")[l][d], 2) for d in range(2)], axis=1))
        put("lbi", np.stack([_fm(f("lru_b_i")[l][d], 2) for d in range(2)], axis=1))
        put("lam", np.stack([_fm(f("lru_lambda")[l][d], 2) for d in range(2)], axis=1))
        put("fcw", np.stack([_fm(f("ffn_conv_w")[l][t], 44) for t in range(3)], axis=-1))
        put("fcb", _fm(f("ffn_conv_b")[l], 44))
    return pv


def prep_x(inp, b):
    x = np.concatenate([np.asarray(inp["ctx"][b], np.float32), np.asarray(inp["x"][b], np.float32)], axis=0)
    return np.ascontiguousarray(x.T.reshape(8, 128, NT))


_CACHE = {}


def kernel(**inputs):
    nb = inputs["x"].shape[0]
    if "nc" not in _CACHE:
        _CACHE["nc"] = build_program()
    nc = _CACHE["nc"]
    sh = prep_shared(inputs)
    in_maps = []
    for b in range(nb):
        m = dict(sh)
        m["pv"] = prep_pv(inputs, b)
        m["xT0"] = prep_x(inputs, b)
        in_maps.append(m)
    res = run_bass_kernel_spmd(nc, in_maps, core_ids=list(range(nb)))
    out = np.empty((nb, NLAT, D), np.float32)
    for b in range(nb):
        o = np.asarray(res.results[b]["outT"], np.float32)
        out[b] = o.reshape(D, NLAT).T
    return out
```

```python
import bisect
import contextlib
import numpy as np
import concourse.bass as bass
import concourse.mybir as mybir
from concourse.bass_utils import run_bass_kernel_spmd

F32 = mybir.dt.float32
BF16 = mybir.dt.bfloat16
AF = mybir.ActivationFunctionType
ALU = mybir.AluOpType

D = 1024
DEPTH = 4
NCTX = 256
NLAT = 4096
NT = NCTX + NLAT
NB = NT // 128
CH = 32
NCH = NT // CH
EPS = 1e-6
DFF = 2816
NC1 = 3200
NFA = 21

PV_G = {"cc": (0, 16), "fng": (16, 8), "lbraw": (24, 8)}
PV_GN = 32
PV_L = {}
_o = 0
for _n, _s in (("gmix", 8), ("gffn", 8), ("adab", 48), ("hgng", 2), ("sink", 8), ("lcw", 8), ("lcb", 2),
               ("lbr", 4), ("lbi", 4), ("lam", 4), ("fcw", 132), ("fcb", 44)):
    PV_L[_n] = (_o, _s)
    _o += _s
PV_LN = _o
NPV = PV_GN + DEPTH * PV_LN
DV_L = {}
_o = 0
for _n, _s in (("lb", 2), ("oml", 2), ("esink", 8), ("c1", 4), ("c2", 4)):
    DV_L[_n] = (_o, _s)
    _o += _s
DV_LN = _o
DV_GN = 16
NDV = DV_GN + DEPTH * DV_LN
CM = {"ident": 0, "blkones": 128, "hgm_f": 256, "hgm_b": 384, "mprev": 512, "mnext": 640, "ones": 768,
      "rm_f": 896, "rm_b": 1408, "chm": 1920}
NCM = 1924


ENGS = ("pe", "act", "dve", "pool", "sp")


class Slot:
    __slots__ = ("w", "r")

    def __init__(self):
        self.w = None
        self.r = []


class Buf:
    def __init__(self, h):
        self.h = h
        self.s = Slot()

    def __getitem__(self, k):
        return self.h[k]


class Op:
    __slots__ = ("eng", "emit", "deps", "needs_inc", "val", "is_dma", "sem", "ndma", "gidx", "barrier")

    def __init__(self, eng, emit, is_dma=False, ndma=1):
        self.eng = eng
        self.emit = emit
        self.deps = []
        self.needs_inc = False
        self.val = None
        self.is_dma = is_dma
        self.sem = None
        self.ndma = ndma
        self.gidx = 0
        self.barrier = False


class Prog:
    def __init__(self, nc, n_dma_sems=64):
        self.nc = nc
        self.all = []
        self.n_dma_sems = n_dma_sems
        self.dma_rr = 0
        self.dma_rr_sw = 0
        self.last = {}

    def op(self, eng, emit, reads=(), writes=(), is_dma=False, ndma=1):
        o = Op(eng, emit, is_dma, ndma)
        o.gidx = len(self.all)
        deps = set()
        for b in reads:
            s = b.s
            if s.w is not None:
                deps.add(s.w)
        for b in writes:
            s = b.s
            if s.w is not None:
                deps.add(s.w)
            deps.update(s.r)
        o.deps = list(deps)
        for d in o.deps:
            d.needs_inc = True
        for b in reads:
            b.s.r.append(o)
        for b in writes:
            b.s.w = o
            b.s.r = []
        if is_dma:
            if eng == "pool":
                o.sem = self.n_dma_sems - 16 + self.dma_rr_sw
                self.dma_rr_sw = (self.dma_rr_sw + 1) % 16
            else:
                o.sem = self.dma_rr
                self.dma_rr = (self.dma_rr + 1) % (self.n_dma_sems - 16)
            o.needs_inc = True
        else:
            self.last[eng] = o
        self.all.append(o)
        return o

    def dma(self, eng, out, in_, reads=(), writes=()):
        return self.op(eng, lambda e: e.dma_start(out=out, in_=in_), reads, writes, is_dma=True)

    def barrier(self):
        o = Op("all", None)
        o.barrier = True
        o.gidx = len(self.all)
        o.deps = list(self.last.values())
        for d in o.deps:
            d.needs_inc = True
        self.all.append(o)

    def build(self):
        nc = self.nc
        with contextlib.ExitStack() as st:
            esem = {e: st.enter_context(nc.semaphore("s_" + e)) for e in ("pe", "act", "dve", "pool")}
            dsem = [st.enter_context(nc.semaphore("d%d" % i)) for i in range(self.n_dma_sems)]
            cnt = {e: 0 for e in esem}
            dcnt = [0] * self.n_dma_sems
            dhist = [[] for _ in range(self.n_dma_sems)]
            bar_d = {}
            for o in self.all:
                if o.barrier:
                    bar_d[o.gidx] = list(dcnt)
                elif o.is_dma:
                    dcnt[o.sem] += 16 * o.ndma
                    o.val = dcnt[o.sem]
                    dhist[o.sem].append((o.gidx, o.val))
                elif o.needs_inc:
                    cnt[o.eng] += 1
                    o.val = cnt[o.eng]
            waited = {e: {} for e in ENGS}
            plans = {e: [] for e in ENGS}

            def addw(eng, ws):
                wl = []
                for key, v in ws.items():
                    if v <= 0 or waited[eng].get(key, 0) >= v:
                        continue
                    waited[eng][key] = v
                    wl.append((key, v))
                return wl

            for o in self.all:
                if o.barrier:
                    ws = {("e", d.eng): d.val for d in o.deps}
                    for i, v in enumerate(bar_d[o.gidx]):
                        ws[("d", i)] = v
                    for e in ENGS:
                        plans[e].append((None, addw(e, dict(ws))))
                    continue
                ws = {}
                if o.is_dma:
                    ws[("d", o.sem)] = o.val - 16 * o.ndma
                for d in o.deps:
                    if d.is_dma:
                        v = d.val
                        key = ("d", d.sem)
                    else:
                        v = d.val
                        key = ("e", d.eng)
                    if ws.get(key, 0) < v:
                        ws[key] = v
                plans[o.eng].append((o, addw(o.eng, ws)))
            fin = [(("d", i), dcnt[i]) for i in range(self.n_dma_sems) if dcnt[i] > 0]

            def semof(key):
                return dsem[key[1]] if key[0] == "d" else esem[key[1]]

            def run(engname, eng):
                for o, wl in plans[engname]:
                    for key, v in wl:
                        eng.wait_ge(semof(key), v)
                    if o is None:
                        continue
                    inst = o.emit(eng)
                    if o.is_dma:
                        inst.then_inc(dsem[o.sem], 16)
                    elif o.needs_inc:
                        if isinstance(inst, (list, tuple)):
                            inst = inst[-1]
                        inst.then_inc(esem[o.eng], 1)
                if engname == "sp":
                    for key, v in fin:
                        eng.wait_ge(semof(key), v)

            with nc.Block() as block:
                @block.tensor
                def _(e):
                    run("pe", e)

                @block.scalar
                def _(e):
                    run("act", e)

                @block.vector
                def _(e):
                    run("dve", e)

                @block.gpsimd
                def _(e):
                    run("pool", e)

                @block.sync
                def _(e):
                    run("sp", e)


class Arena:
    def __init__(self, nc, base, size):
        self.nc, self.base, self.size, self.top, self.n = nc, base, size, 0, 0

    def alloc(self, free, dt):
        nb = int(np.prod(free)) * (2 if dt == BF16 else 4)
        off = (self.top + 31) // 32 * 32
        assert off + nb <= self.size, ("SBUF arena overflow", off + nb, self.size)
        self.top = off + nb
        self.n += 1
        h = self.nc.alloc_sbuf_tensor_at("t%d" % self.n, [128] + list(free), dt, offset=self.base + off)
        return Buf(h)


class Ring:
    def __init__(self, bufs):
        self.b, self.i = bufs, 0

    def get(self):
        b = self.b[self.i]
        self.i = (self.i + 1) % len(self.b)
        return b


class _Stop(Exception):
    pass


def build_program(nlayers=DEPTH, debug=False, stop_after=None):
    nc = bass.Bass("TRN2", target_bir_lowering=False)
    dbgk = "ExternalOutput" if debug else "Internal"

    def din(name, shape, dt=F32):
        return nc.dram_tensor(name, list(shape), dt, kind="ExternalInput").ap()

    xT0 = din("xT0", [8, 128, NT])
    pv_d = din("pv", [128, NPV])
    cs_d = din("cs", [2, 128, NT])
    cm_d = din("cmask", [128, NCM])
    adaw = din("adaw", [DEPTH, 128, 8, 6 * D])
    w1_d = din("w1", [DEPTH, 128, 8, NC1])
    w2_d = din("w2", [DEPTH, 128, 8, D])
    w3_d = din("w3", [DEPTH, 128, 8, 2 * DFF])
    w4_d = din("w4", [DEPTH, 128, 22, D])
    wl_d = din("wlru", [DEPTH, 128, 8, 128])
    outT = nc.dram_tensor("outT", [8, 128, NLAT], F32, kind="ExternalOutput").ap()
    XA = nc.dram_tensor("XA", [8, 128, NT], F32, kind=dbgk).ap()
    XB = nc.dram_tensor("XB", [8, 128, NT], F32, kind=dbgk).ap()
    FA = nc.dram_tensor("FA", [NFA, 128, NT], BF16, kind=dbgk).ap()
    XL = nc.dram_tensor("XL", [2, 128, NT], F32, kind=dbgk).ap()
    VT = nc.dram_tensor("VT", [3, 128, NB, 128], BF16, kind=dbgk).ap()
    MX = nc.dram_tensor("MX", [8, 128, NT], BF16, kind=dbgk).ap()
    DVD = nc.dram_tensor("DVD", [128, NDV], F32, kind=dbgk).ap()
    DBG = nc.dram_tensor("DBG", [6, 128, 512], F32, kind=dbgk).ap()
    dXA, dXB, dFA, dXL, dVT, dMX = (Buf(None) for _ in range(6))

    base = (nc.sbuf_base + 31) // 32 * 32
    size = nc.sbuf_top - base - 64
    nc.alloc_sbuf_tensor("arena", [128, size // 4], F32)
    AR = Arena(nc, base, size)
    psall = nc.alloc_psum_tensor("psall", [128, 8 * 512], F32)
    banks = [Buf(psall[:, i * 512:(i + 1) * 512]) for i in range(8)]
    bring = Ring(banks)

    def span(i0, n):
        return psall[:, i0 * 512:(i0 + n) * 512]
    P = Prog(nc)

    def ACT(out, in_, func, reads, writes, scale=None, bias=None):
        kw = {}
        if scale is not None:
            kw["scale"] = scale
        if bias is not None:
            kw["bias"] = bias
        return P.op("act", lambda e: e.activation(out=out, in_=in_, func=func, **kw), reads, writes)

    def TT(eng, out, a, b, op, reads, writes):
        return P.op(eng, lambda e: e.tensor_tensor(out=out, in0=a, in1=b, op=op), reads, writes)

    def TS(eng, out, a, s1, s2, op0, op1, reads, writes):
        if op1 is None:
            return P.op(eng, lambda e: e.tensor_scalar(out=out, in0=a, scalar1=s1, scalar2=None, op0=op0), reads, writes)
        return P.op(eng, lambda e: e.tensor_scalar(out=out, in0=a, scalar1=s1, scalar2=s2, op0=op0, op1=op1), reads, writes)

    def STT(out, a, s, b, op0, op1, reads, writes):
        return P.op("dve", lambda e: e.scalar_tensor_tensor(out=out, in0=a, scalar=s, in1=b, op0=op0, op1=op1), reads, writes)

    def CP(eng, out, in_, reads, writes):
        return P.op(eng, lambda e: e.tensor_copy(out=out, in_=in_), reads, writes)

    def SCAN(out, d0, d1, initial, reads, writes):
        return P.op("dve", lambda e: e.tensor_tensor_scan(out=out, data0=d0, data1=d1, initial=initial,
                                                          op0=ALU.mult, op1=ALU.add), reads, writes)

    def MEMSET(eng, ap, val, writes):
        return P.op(eng, lambda e: e.memset(ap, val), [], writes)

    def MM(out, pairs, reads, writes):
        def emit(e):
            n = len(pairs)
            last = None
            for i, (l, r) in enumerate(pairs):
                last = e.matmul(out, lhsT=l, rhs=r, start=(i == 0), stop=(i == n - 1))
            return last
        return P.op("pe", emit, reads, writes)

    pv = AR.alloc([NPV], F32)
    dv = AR.alloc([NDV], F32)
    cmb = AR.alloc([NCM], BF16)
    dec = [AR.alloc([NCH], F32) for _ in range(4)]
    mv = [AR.alloc([128], F32) for _ in range(DEPTH)]
    persist_top = AR.top

    def pvl(l, name, sub=None):
        o, n = PV_L[name]
        o += PV_GN + l * PV_LN
        return pv.h[:, o:o + n]

    def dvl(l, name):
        o, n = DV_L[name]
        o += DV_GN + l * DV_LN
        return dv.h[:, o:o + n]

    def pvg(name):
        o, n = PV_G[name]
        return pv.h[:, o:o + n]

    def cmB(name, n=128):
        return cmb.h[:, CM[name]:CM[name] + n]


    P.op("pool", lambda e: e.memset(dv.h[:], 0.0), [], [dv])
    P.dma("sp", pv.h[:], pv_d, writes=[pv])
    P.dma("pool", cmb.h[:], cm_d, writes=[cmb])
    tA = AR.alloc([64], F32)
    tB = AR.alloc([64], F32)
    ACT(tA.h[:, 0:16], pvg("cc"), AF.Exp, [pv], [tA], scale=-1.0)
    ACT(tB.h[:, 0:16], tA.h[:, 0:16], AF.Ln, [tA], [tB], bias=1.0)
    ACT(tA.h[:, 0:16], tB.h[:, 0:16], AF.Exp, [tB], [tA], scale=-1.0)
    TT("dve", dv.h[:, 0:16], tA.h[:, 0:16], pvg("cc"), ALU.mult, [tA, pv], [dv])
    ACT(tA.h[:, 0:8], pvg("lbraw"), AF.Exp, [pv, tA], [tA])
    P.op("dve", lambda e: e.tensor_reduce(out=tB.h[:, 0:2], in_=tA.h[:, 0:8].rearrange("p (c l) -> p c l", l=4),
                                          axis=mybir.AxisListType.X, op=ALU.add), [tA], [tB])
    P.op("dve", lambda e: e.reciprocal(out=tB.h[:, 2:4], in_=tB.h[:, 0:2]), [tB], [tB])
    TT("dve", tA.h[:, 8:16].rearrange("p (c l) -> p c l", l=4), tA.h[:, 0:8].rearrange("p (c l) -> p c l", l=4),
       tB.h[:, 2:4].unsqueeze(2).to_broadcast([128, 2, 4]), ALU.mult, [tA, tB], [tA])
    pl = tA.h[:, 8:16].rearrange("p (c l) -> p c l", l=4)
    P.op("dve", lambda e: e.memset(dvl(0, "lb"), 0.0), [], [dv])
    CP("dve", dvl(1, "lb"), pl[:, :, 1], [tA], [dv])
    TT("dve", dvl(2, "lb"), dvl(1, "lb"), pl[:, :, 2], ALU.add, [tA, dv], [dv])
    TT("dve", dvl(3, "lb"), dvl(2, "lb"), pl[:, :, 3], ALU.add, [tA, dv], [dv])
    for l in range(DEPTH):
        TS("dve", dvl(l, "oml"), dvl(l, "lb"), -1.0, 1.0, ALU.mult, ALU.add, [dv], [dv])
        ACT(dvl(l, "esink"), pvl(l, "sink"), AF.Exp, [pv, dv], [dv])
        ACT(tB.h[:, 8:12], pvl(l, "lam"), AF.Exp, [pv, tB], [tB], scale=-1.0)
        ACT(tB.h[:, 12:16], tB.h[:, 8:12], AF.Ln, [tB], [tB], bias=1.0)
        TS("dve", dvl(l, "c1"), tB.h[:, 12:16], -8.0, None, ALU.mult, None, [tB, dv], [dv])
        TS("dve", dvl(l, "c2"), tB.h[:, 12:16], -16.0, None, ALU.mult, None, [tB, dv], [dv])
    def compute_mod_gen(l, wb2, pb):
        def load(mg):
            wb = wb2.get()
            P.dma("sp" if mg % 2 == 0 else "pool", wb.h[:], adaw[l, :, :, mg * 512:(mg + 1) * 512], writes=[wb])
            return wb
        wnext = load(0)
        for mg in range(12):
            wb = wnext
            if mg + 1 < 12:
                wnext = load(mg + 1)
            for mi in range(4):
                m = mg * 4 + mi
                MM(pb.h[:, 2 * m:2 * m + 2],
                   [(wb.h[:, k, mi * 128:(mi + 1) * 128], dv.h[:, 2 * k:2 * k + 2]) for k in range(8)],
                   [wb, dv], [pb])
            yield
        modv = mv[l].h[:, 0:96].rearrange("p (m j) -> p m j", j=2)
        TT("dve", modv, pb.h[:, 0:96].rearrange("p (m j) -> p m j", j=2),
           pvl(l, "adab").unsqueeze(2).to_broadcast([128, 48, 2]), ALU.add, [pb, pv], [mv[l]])
        for o_, c0, gn in ((96, 8, "gmix"), (112, 32, "gffn")):
            gv = mv[l].h[:, o_:o_ + 16].rearrange("p (k j) -> p k j", j=2)
            TS("dve", gv, modv[:, c0:c0 + 8, :], 1.0, None, ALU.add, None, [mv[l]], [mv[l]])
            TT("dve", gv, gv, pvl(l, gn).unsqueeze(2).to_broadcast([128, 8, 2]), ALU.mult, [mv[l], pv], [mv[l]])
        yield

    mark = AR.top
    for _ in compute_mod_gen(0, Ring([AR.alloc([8, 512], F32) for _ in range(2)]), bring.get()):
        pass
    if debug:
        P.dma("sp", DVD, dv.h[:], reads=[dv])
    P.barrier()
    AR.top = mark

    def modcol(l, chunk, j):
        o = 2 * chunk + j
        return mv[l].h[:, o:o + 1]

    def gscol(l, nm, k, j):
        o = (96 if nm == "gs1" else 112) + 2 * k + j
        return mv[l].h[:, o:o + 1]

    def rmsnorm_mod(l, xt_b, T, gsn, shc, j, sqb, lnb, rsb, hnr, hTb):
        ACT(sqb.h[:, :, :T], xt_b.h[:, :, :T], AF.Square, [xt_b], [sqb])
        pb = bring.get()
        MM(pb.h[:, :T], [(cmB("ones"), sqb.h[:, k, :T]) for k in range(8)], [sqb, cmb], [pb])
        ACT(lnb.h[:, :T], pb.h[:, :T], AF.Ln, [pb], [lnb], scale=1.0 / D, bias=EPS)
        ACT(rsb.h[:, :T], lnb.h[:, :T], AF.Exp, [lnb], [rsb], scale=-0.5)
        for k in range(8):
            hn = hnr.get()
            STT(hn.h[:, :T], xt_b.h[:, k, :T], gscol(l, gsn, k, j), rsb.h[:, :T], ALU.mult, ALU.mult,
                [xt_b, rsb, mv[l]], [hn])
            ACT(hTb.h[:, k, :T], hn.h[:, :T], AF.Identity, [hn, mv[l]], [hTb], bias=modcol(l, shc + k, j))

    Xin, Xmid = (XA, dXA), (XB, dXB)
    first_src = (xT0, Buf(None))

    def layer_body(l):
        src, dsrc = (first_src if l == 0 else Xin)
        AR.top = persist_top
        W1 = AR.alloc([8, NC1], BF16)
        for k in range(8):
            P.dma("pool", W1.h[:, k, :], w1_d[l, :, k, :], writes=[W1])
        xtr = Ring([AR.alloc([8, 512], F32) for _ in range(2)])
        csr = Ring([AR.alloc([2, 512], F32) for _ in range(2)])
        sqb = AR.alloc([8, 512], BF16)
        lnb = AR.alloc([512], F32)
        rsb = AR.alloc([512], F32)
        hnr = Ring([AR.alloc([512], F32) for _ in range(2)])
        hTb = AR.alloc([8, 512], BF16)
        tmr = Ring([AR.alloc([512], F32) for _ in range(6)])
        hgt = {X: [AR.alloc([512], F32) for _ in range(2 if X in ("q", "g") else 4)] for X in ("q", "g", "f0", "f1")}
        stg = AR.alloc([NFA, 512], BF16)
        xls = AR.alloc([2, 512], F32)
        vts = AR.alloc([3, 4, 128], BF16)
        sqq = AR.alloc([512], F32)
        tiles = [(0, 256)] + [(256 + 512 * i, 512) for i in range(8)]
        hTr = Ring([hTb, AR.alloc([8, 512], BF16)])

        def prepA(tile):
            n0, T = tile
            j = 1 if n0 < NCTX else 0
            xt_b = xtr.get()
            P.dma("sp", xt_b.h[:, :, :T], src.rearrange("c p n -> p c n")[:, :, n0:n0 + T], reads=[dsrc], writes=[xt_b])
            cs_b = csr.get()
            P.dma("sp", cs_b.h[:, :, :T], cs_d.rearrange("c p n -> p c n")[:, :, n0:n0 + T], writes=[cs_b])
            hT_ = hTr.get()
            rmsnorm_mod(l, xt_b, T, "gs1", 0, j, sqb, lnb, rsb, hnr, hT_)
            return cs_b, hT_

        nxtA = prepA(tiles[0])
        for ti_, (n0, T) in enumerate(tiles):
            cs_b, hTb = nxtA

            def proj(c):
                pb = bring.get()
                MM(pb.h[:, :T], [(W1.h[:, k, c * 128:(c + 1) * 128], hTb.h[:, k, :T]) for k in range(8)], [W1, hTb], [pb])
                return pb

            def silu_from(pb):
                t1, t2 = tmr.get(), tmr.get()
                ACT(t1.h[:, :T], pb.h[:, :T], AF.Exp, [pb], [t1], scale=-1.0)
                ACT(t2.h[:, :T], t1.h[:, :T], AF.Ln, [t1], [t2], bias=1.0)
                ACT(t1.h[:, :T], t2.h[:, :T], AF.Exp, [t2], [t1], scale=-1.0)
                return t1

            for hp in range(2):
                chains = ("q", "g", "f0", "f1")
                ccol = {"q": hp, "g": 6 + hp, "f0": 2 + hp, "f1": 4 + hp}
                pbk = {X: proj(ccol[X]) for X in chains}
                t1 = {X: hgt[X][0] for X in chains}
                t2 = {X: hgt[X][1] for X in chains}
                for X in chains:
                    ACT(t1[X].h[:, :T], pbk[X].h[:, :T], AF.Exp, [pbk[X]], [t1[X]], scale=-1.0)
                for X in chains:
                    ACT(t2[X].h[:, :T], t1[X].h[:, :T], AF.Ln, [t1[X]], [t2[X]], bias=1.0)
                for X in chains:
                    ACT(t1[X].h[:, :T], t2[X].h[:, :T], AF.Exp, [t2[X]], [t1[X]], scale=-1.0)
                sq_ = sqq
                TT("dve", sq_.h[:, :T], pbk["q"].h[:, :T], t1["q"].h[:, :T], ALU.mult, [pbk["q"], t1["q"]], [sq_])
                TT("dve", t2["g"].h[:, :T], pbk["g"].h[:, :T], t1["g"].h[:, :T], ALU.mult, [pbk["g"], t1["g"]], [t2["g"]])
                for dr in range(2):
                    X = "f%d" % dr
                    TS("dve", t2[X].h[:, :T], t1[X].h[:, :T], dvl(l, "oml")[:, hp:hp + 1], dvl(l, "lb")[:, hp:hp + 1],
                       ALU.mult, ALU.add, [t1[X], dv], [t2[X]])
                ACT(stg.h[:, 12 + hp, :T], t2["g"].h[:, :T], AF.Identity, [t2["g"], pv], [stg], scale=pvl(l, "hgng")[:, hp:hp + 1])
                for dr in range(2):
                    X = "f%d" % dr
                    ACT(t1[X].h[:, :T], t2[X].h[:, :T], AF.Ln, [t2[X]], [t1[X]])
                t3 = {}
                for dr in range(2):
                    X = "f%d" % dr
                    t3[X] = hgt[X][2]
                    if dr == 0:
                        SCAN(t3[X].h[:, :T], cmB("rm_f", 512)[:, :T], t1[X].h[:, :T], 0.0, [t1[X], cmb], [t3[X]])
                    else:
                        SCAN(t3[X].h[:, :T][:, ::-1], cmB("rm_b", 512)[:, :T][:, ::-1], t1[X].h[:, :T][:, ::-1], 0.0,
                             [t1[X], cmb], [t3[X]])
                t4 = {}
                for dr in range(2):
                    X = "f%d" % dr
                    t4[X] = hgt[X][3]
                    ACT(t1[X].h[:, :T], t3[X].h[:, :T], AF.Exp, [t3[X]], [t1[X]])
                    ACT(t4[X].h[:, :T], t3[X].h[:, :T], AF.Exp, [t3[X]], [t4[X]], scale=-1.0)
                    ACT(t2[X].h[:, :T], t2[X].h[:, :T], AF.Identity, [t2[X]], [t2[X]], scale=-1.0, bias=1.0)
                nchk = T // CH
                for dr in range(2):
                    X = "f%d" % dr
                    idx = hp * 2 + dr
                    TT("dve", stg.h[:, 4 + idx, :T], t2[X].h[:, :T], t4[X].h[:, :T], ALU.mult, [t2[X], t4[X]], [stg])
                    eend = t1[X].h[:, :T].rearrange("p (c j) -> p c j", j=CH)
                    eend = eend[:, :, CH - 1:CH] if dr == 0 else eend[:, :, 0:1]
                    TT("dve", stg.h[:, 8 + idx, :T].rearrange("p (c j) -> p c j", j=CH),
                       stg.h[:, 4 + idx, :T].rearrange("p (c j) -> p c j", j=CH), eend.to_broadcast([128, nchk, CH]), ALU.mult,
                       [t1[X], stg], [stg])
                    TT("dve", stg.h[:, idx, :T], sq_.h[:, :T], t1[X].h[:, :T], ALU.mult, [sq_, t1[X]], [stg])
                    ACT(dec[idx].h[:, n0 // CH:n0 // CH + nchk].unsqueeze(2), eend, AF.Copy, [t1[X]], [dec[idx]])
            if ti_ + 1 < len(tiles):
                nxtA = prepA(tiles[ti_ + 1])
            for ci, (cq, cr) in enumerate(((8, 12), (9, 13), (10, 14), (11, 15), (16, 17))):
                pq = proj(cq)
                pr = proj(cr)
                t1, t2 = tmr.get(), tmr.get()
                TT("dve", t1.h[:, :T], pq.h[:, :T], cs_b.h[:, 0, :T], ALU.mult, [pq, cs_b], [t1])
                TT("dve", t2.h[:, :T], pr.h[:, :T], cs_b.h[:, 1, :T], ALU.mult, [pr, cs_b], [t2])
                TT("dve", stg.h[:, 14 + ci, :T], t1.h[:, :T], t2.h[:, :T], ALU.add, [t1, t2], [stg])
            for jj in range(2):
                px = proj(18 + jj)
                ACT(xls.h[:, jj, :T], px.h[:, :T], AF.Identity, [px], [xls])
                py = proj(20 + jj)
                t1, t2, t3 = tmr.get(), tmr.get(), tmr.get()
                ACT(t1.h[:, :T], py.h[:, :T], AF.Square, [py], [t1])
                ACT(t2.h[:, :T], t1.h[:, :T], AF.Identity, [t1], [t2], scale=0.044715 * 1.5957691216, bias=1.5957691216)
                TT("dve", t3.h[:, :T], py.h[:, :T], t2.h[:, :T], ALU.mult, [py, t2], [t3])
                ACT(t1.h[:, :T], t3.h[:, :T], AF.Exp, [t3], [t1], scale=-1.0)
                ACT(t2.h[:, :T], t1.h[:, :T], AF.Ln, [t1], [t2], bias=1.0)
                ACT(t1.h[:, :T], t2.h[:, :T], AF.Exp, [t2], [t1], scale=-1.0)
                TT("dve", stg.h[:, 19 + jj, :T], py.h[:, :T], t1.h[:, :T], ALU.mult, [py, t1], [stg])
            for b in range(T // 128):
                pb = bring.get()
                MM(pb.h[:, 0:384], [(hTb.h[:, k, b * 128:(b + 1) * 128], W1.h[:, k, 2816:3200]) for k in range(8)],
                   [W1, hTb], [pb])
                ACT(vts.h[:, :, b, :], pb.h[:, 0:384].rearrange("p (g c) -> p g c", c=128), AF.Identity, [pb], [vts])
            nb_ = T // 128
            P.dma("sp", FA.rearrange("c p n -> p c n")[:, :, n0:n0 + T], stg.h[:, :, :T], reads=[stg], writes=[dFA])
            P.dma("sp", XL.rearrange("c p n -> p c n")[:, :, n0:n0 + T], xls.h[:, :, :T], reads=[xls], writes=[dXL])
            P.dma("sp", VT.rearrange("g p b c -> p g b c")[:, :, n0 // 128:n0 // 128 + nb_, :], vts.h[:, :, :nb_, :], reads=[vts], writes=[dVT])
        P.barrier()
        if stop_after == ("A", l):
            raise _Stop()
        AR.top = persist_top
        mask4 = AR.alloc([4, 128], BF16)
        for q4 in range(4):
            ACT(mask4.h[:, q4, :], cmB("hgm_f" if q4 % 2 == 0 else "hgm_b"), AF.Copy, [cmb], [mask4])
        markH = AR.top
        for hp in range(2):
            AR.top = markH
            vv = AR.alloc([NB, 128], BF16)
            P.dma("sp", vv.h[:], VT[hp], reads=[dVT], writes=[vv])
            Ul = AR.alloc([64, 128], F32)
            Uc = AR.alloc([64, 8], F32)
            drl = AR.alloc([64, 128], F32)
            drc = AR.alloc([64, 8], F32)
            Scf = AR.alloc([64, 8], F32)
            Scb = [AR.alloc([8, 128], BF16) for _ in range(2)]
            Slb = [AR.alloc([128, 128], BF16) for _ in range(2)]
            for bb in Scb + Slb:
                MEMSET("dve", bb.h[:], 0.0, [bb])
            markU = AR.top
            Slt = AR.alloc([64, 128], BF16)
            ko = AR.alloc([NT], BF16)
            kotr = Ring([AR.alloc([4, 4, 128], BF16) for _ in range(2)])
            for dr in range(2):
                idx = hp * 2 + dr
                P.dma("sp", ko.h[:], FA[8 + idx], reads=[dFA], writes=[ko])
                bring.b, bring.i = banks[6:8], 0
                pairs = Ring([(0, banks[0:2]), (2, banks[2:4]), (4, banks[4:6])])
                groups = [(0, 2)] + [(b0, 4) for b0 in range(2, NB, 4)]
                def stage_t(b0, nbk, ko=ko):
                    pt = bring.get()
                    ptb = pt.h.bitcast(BF16)

                    def emit_t(e, ptb=ptb, b0=b0, nbk=nbk, ko=ko):
                        last = None
                        for i in range(nbk):
                            last = e.transpose(ptb[:, i * 128:(i + 1) * 128], ko.h[:, (b0 + i) * 128:(b0 + i + 1) * 128], cmB("ident"))
                        return last
                    P.op("pe", emit_t, [ko, cmb], [pt])
                    kot = kotr.get()
                    TT("dve", kot.h[:, :nbk], ptb[:, 0:nbk * 128].rearrange("p (b k) -> p b k", k=128).unsqueeze(2).to_broadcast([128, nbk, 4, 128]),
                       cmB("chm", 4).unsqueeze(1).unsqueeze(3).to_broadcast([128, nbk, 4, 128]), ALU.mult, [pt, cmb], [kot])
                    return kot

                kot_next = stage_t(*groups[0])
                for gi, (b0, nbk) in enumerate(groups):
                    kot = kot_next
                    if gi + 1 < len(groups):
                        kot_next = stage_t(*groups[gi + 1])

                    pi0, pbufs = pairs.get()
                    pap = span(pi0, 2)

                    def emit_u(e, pap=pap, kot=kot, b0=b0, nbk=nbk, vv=vv):
                        last = None
                        for i in range(nbk):
                            for c in range(4):
                                for h in range(2):
                                    last = e.matmul(pap[h * 64:(h + 1) * 64, (i * 4 + c) * 64:(i * 4 + c + 1) * 64],
                                                    lhsT=kot.h[:, i, c, h * 64:(h + 1) * 64],
                                                    rhs=vv.h[:, b0 + i, h * 64:(h + 1) * 64], start=True, stop=True)
                        return last
                    P.op("pe", emit_u, [kot, vv], pbufs)
                    if b0 < 2:
                        dstb, c0 = Uc, 0
                    else:
                        dstb, c0 = Ul, 4 * (b0 - 2)
                    ncc = nbk * 4
                    for h in range(2):
                        srcv = pap[h * 64:(h + 1) * 64, 0:ncc * 64].rearrange("p (c e) -> p e c", e=64)
                        dstv = dstb.h[h * 64:(h + 1) * 64, :, c0:c0 + ncc]
                        if h == 0:
                            ACT(dstv, srcv, AF.Copy, pbufs, [dstb])
                        else:
                            CP("dve", dstv, srcv, pbufs, [dstb])
                bring.b, bring.i = banks, 0
                if stop_after == ("H1", l):
                    raise _Stop()
                dcv = dec[idx].h
                ACT(drc.h[:], dcv[:, 0:8].unsqueeze(1).to_broadcast([128, 64, 8]), AF.Copy, [dec[idx]], [drc])
                ACT(drl.h[:], dcv[:, 8:NCH].unsqueeze(1).to_broadcast([128, 64, 128]), AF.Copy, [dec[idx]], [drl])
                fc, fl = (0, 0) if dr == 0 else (7, 127)
                MEMSET("dve", drc.h[:, :, fc:fc + 1], 0.0, [drc])
                MEMSET("dve", drl.h[:, :, fl:fl + 1], 0.0, [drl])
                flat = lambda bf: bf.h[:].rearrange("p e c -> p (e c)")
                rv = (lambda ap: ap) if dr == 0 else (lambda ap: ap[:, ::-1])
                SCAN(rv(flat(Scf)), rv(flat(drc)), rv(flat(Uc)), 0.0, [drc, Uc], [Scf])
                for h in range(2):
                    ph = slice(h * 64, (h + 1) * 64)
                    ACT(Scb[dr].h[ph, :, h * 64:(h + 1) * 64], Scf.h[ph].rearrange("p e c -> p c e"), AF.Copy, [Scf], [Scb[dr]])
                lastc = 7 if dr == 0 else 0
                STT(Ul.h[:, :, fl], Scf.h[:, :, lastc], dcv[:, 8 + fl:9 + fl], Ul.h[:, :, fl], ALU.mult, ALU.add,
                    [Scf, dec[idx], Ul], [Ul])
                SCAN(rv(flat(Slt)), rv(flat(drl)), rv(flat(Ul)), 0.0, [drl, Ul], [Slt])
                for h in range(2):
                    ph = slice(h * 64, (h + 1) * 64)
                    if h == 0:
                        ACT(Slb[dr].h[ph, :, h * 64:(h + 1) * 64], Slt.h[ph].rearrange("p e c -> p c e"), AF.Copy, [Slt], [Slb[dr]])
                    else:
                        CP("dve", Slb[dr].h[ph, :, h * 64:(h + 1) * 64], Slt.h[ph].rearrange("p e c -> p c e"), [Slt], [Slb[dr]])
            if stop_after == ("H2", l):
                raise _Stop()
            P.barrier()
            AR.top = markU
            qkr = Ring([AR.alloc([2, 512], BF16) for _ in range(2)])
            kbr = Ring([AR.alloc([2, 2, 512], BF16) for _ in range(2)])
            for bb in kbr.b:
                MEMSET("dve", bb.h[:], 0.0, [bb])
            gsr = Ring([AR.alloc([512], BF16) for _ in range(2)])
            scr = Ring([AR.alloc([4, 4, 128], BF16) for _ in range(2)])
            osq = AR.alloc([512], BF16)
            lnh = AR.alloc([512], F32)
            rsh = AR.alloc([512], F32)
            th = AR.alloc([512], F32)
            yr = Ring([AR.alloc([512], BF16) for _ in range(2)])

            def state_prev(dr, cg):
                if dr == 0:
                    if cg == 0:
                        return None, None
                    if cg <= 8:
                        return Scb[0], Scb[0].h[:, cg - 1, :]
                    return Slb[0], Slb[0].h[:, cg - 9, :]
                if cg == 7:
                    return None, None
                if cg < 7:
                    return Scb[1], Scb[1].h[:, cg + 1, :]
                if cg == NCH - 1:
                    return Scb[1], Scb[1].h[:, 0, :]
                return Slb[1], Slb[1].h[:, cg - 8 + 1, :]

            def partA(n0, T):
                qk = qkr.get()
                kbp = kbr.get()
                for dr in range(2):
                    P.dma("sp", qk.h[:, dr, :T], FA[hp * 2 + dr][:, n0:n0 + T], reads=[dFA], writes=[qk])
                    for h in range(2):
                        ph = slice(h * 64, (h + 1) * 64)
                        P.dma("act", kbp.h[ph, h, dr, :T], FA[4 + hp * 2 + dr][ph, n0:n0 + T], reads=[dFA], writes=[kbp])
                gsb = gsr.get()
                P.dma("sp", gsb.h[:, :T], FA[12 + hp][:, n0:n0 + T], reads=[dFA], writes=[gsb])
                po = bring.get()
                nbk = T // 128
                qb_ = banks[0:4]
                qap = span(0, nbk)

                def emit_s(e, qap=qap, qk=qk, kbp=kbp, nbk=nbk):
                    last = None
                    for bi in range(nbk):
                        cs_ = slice(bi * 128, (bi + 1) * 128)
                        for h in range(2):
                            for dr in range(2):
                                o0 = bi * 512 + (h * 2 + dr) * 128
                                last = e.matmul(qap[:, o0:o0 + 128], lhsT=kbp.h[:, h, dr, cs_], rhs=qk.h[:, dr, cs_], start=True, stop=True)
                    return last
                P.op("pe", emit_s, [qk, kbp], qb_[:nbk])
                sc = scr.get()
                TT("dve", sc.h[:, :nbk], qap.rearrange("p (b a k) -> p b a k", a=4, k=128),
                   mask4.h[:].unsqueeze(1).to_broadcast([128, nbk, 4, 128]), ALU.mult, qb_[:nbk] + [mask4], [sc])
                seq = []
                for bi in range(nbk):
                    b = n0 // 128 + bi
                    cs_ = slice(bi * 128, (bi + 1) * 128)
                    for dr in range(2):
                        for h in range(2):
                            ph = slice(h * 64, (h + 1) * 64)
                            seq.append((po.h[ph, cs_], vv.h[:, b, h * 64:(h + 1) * 64], sc.h[:, bi, h * 2 + dr, :], dr == 0, False))
                        for c in range(4):
                            _, sap = state_prev(dr, 4 * b + c)
                            if sap is None:
                                continue
                            c0 = bi * 128 + c * 32
                            seq.append((po.h[:, c0:c0 + 32], sap, qk.h[:, dr, c0:c0 + 32], False, False))
                    seq[-1] = seq[-1][:4] + (True,)

                return dict(n0=n0, T=T, po=po, seq=seq, sc=sc, qk=qk, gsb=gsb)

            def partB1(cx):
                seq, sc, qk, po = cx['seq'], cx['sc'], cx['qk'], cx['po']
                def emit_o(e, seq=seq):
                    last = None
                    for (o_, l_, r_, st_, sp_) in seq:
                        last = e.matmul(o_, lhsT=l_, rhs=r_, start=st_, stop=sp_)
                    return last
                P.op("pe", emit_o, [sc, vv, qk, Scb[0], Scb[1], Slb[0], Slb[1]], [po])

            def partB2(cx):
                n0, T, po, gsb = cx['n0'], cx['T'], cx['po'], cx['gsb']
                ACT(osq.h[:, :T], po.h[:, :T], AF.Square, [po], [osq])
                pss = bring.get()
                MM(pss.h[:, :T], [(cmB("blkones"), osq.h[:, :T])], [osq, cmb], [pss])
                ACT(lnh.h[:, :T], pss.h[:, :T], AF.Ln, [pss], [lnh], scale=1.0 / 64, bias=EPS)
                ACT(rsh.h[:, :T], lnh.h[:, :T], AF.Exp, [lnh], [rsh], scale=-0.5)
                TT("dve", th.h[:, :T], po.h[:, :T], rsh.h[:, :T], ALU.mult, [po, rsh], [th])
                yb = yr.get()
                TT("dve", yb.h[:, :T], th.h[:, :T], gsb.h[:, :T], ALU.mult, [th, gsb], [yb])
                P.dma("pool", MX[hp][:, n0:n0 + T], yb.h[:, :T], reads=[yb], writes=[dMX])

            bring.b, bring.i = banks[4:8], 0
            cx = partA(*tiles[0])
            for ti_ in range(len(tiles)):
                partB1(cx)
                nx = partA(*tiles[ti_ + 1]) if ti_ + 1 < len(tiles) else None
                partB2(cx)
                cx = nx
            bring.b, bring.i = banks, 0
            P.barrier()
        if stop_after == ("H", l):
            raise _Stop()
        AR.top = persist_top
        wlb = AR.alloc([8, 128], BF16)
        P.dma("pool", wlb.h[:], wl_d[l], writes=[wlb])
        negb = AR.alloc([8], F32)
        TS("dve", negb.h[:, 0:4], pvl(l, "lbr"), -1.0, None, ALU.mult, None, [pv], [negb])
        TS("dve", negb.h[:, 4:8], pvl(l, "lbi"), -1.0, None, ALU.mult, None, [pv], [negb])
        xp = AR.alloc([NT + 8], F32)
        xc = AR.alloc([NT], F32)
        xcb = AR.alloc([NT], BF16)
        gyb = AR.alloc([NT], BF16)
        hs = AR.alloc([NT], F32)
        hb = AR.alloc([NT], F32)
        a_ = AR.alloc([NT], F32)
        u_ = AR.alloc([NT], F32)
        ym = AR.alloc([NT], BF16)
        tl = Ring([AR.alloc([512], F32) for _ in range(8)])
        modgen = None
        if l + 1 < nlayers:
            bring.b, bring.i = banks[0:7], 0
            modgen = compute_mod_gen(l + 1, Ring([AR.alloc([8, 512], F32) for _ in range(2)]), banks[7])
        lstep = [0]
        for jj in range(2):
            MEMSET("dve", xp.h[:], 0.0, [xp])
            P.dma("sp", xp.h[:, 2:2 + NCTX], XL[jj][:, 0:NCTX], reads=[dXL], writes=[xp])
            P.dma("sp", xp.h[:, 261:261 + NLAT], XL[jj][:, NCTX:NT], reads=[dXL], writes=[xp])
            P.dma("sp", gyb.h[:], FA[19 + jj], reads=[dFA], writes=[gyb])
            lcw = pvl(l, "lcw")
            lcb = pvl(l, "lcb")
            for (off, n, d0) in ((0, NCTX, 0), (259, NLAT, NCTX)):
                for (s0, sn) in [(i, min(1024, n - i)) for i in range(0, n, 1024)]:
                    dst = xc.h[:, d0 + s0:d0 + s0 + sn]
                    TS("dve", dst, xp.h[:, off + s0:off + s0 + sn], lcw[:, jj * 4:jj * 4 + 1], lcb[:, jj:jj + 1], ALU.mult, ALU.add,
                       [xp, pv], [xc])
                    for k in range(1, 4):
                        STT(dst, xp.h[:, off + s0 + k:off + s0 + k + sn], lcw[:, jj * 4 + k:jj * 4 + k + 1], dst, ALU.mult, ALU.add,
                            [xp, pv, xc], [xc])
            ACT(xcb.h[:], xc.h[:], AF.Copy, [xc], [xcb])
            for dr in range(2):
                for (n0, T) in tiles:
                    pr, pi = bring.get(), bring.get()
                    MM(pr.h[:, :T], [(wlb.h[:, dr * 4 + 0 + jj, :], xcb.h[:, n0:n0 + T])], [wlb, xcb], [pr])
                    MM(pi.h[:, :T], [(wlb.h[:, dr * 4 + 2 + jj, :], xcb.h[:, n0:n0 + T])], [wlb, xcb], [pi])

                    def sig(pb, bcol):
                        t1, t2 = tl.get(), tl.get()
                        ACT(t1.h[:, :T], pb.h[:, :T], AF.Exp, [pb, negb], [t1], scale=-1.0, bias=negb.h[:, bcol:bcol + 1])
                        ACT(t2.h[:, :T], t1.h[:, :T], AF.Ln, [t1], [t2], bias=1.0)
                        ACT(t1.h[:, :T], t2.h[:, :T], AF.Exp, [t2], [t1], scale=-1.0)
                        return t1
                    r = sig(pr, dr * 2 + jj)
                    ig = sig(pi, 4 + dr * 2 + jj)
                    ACT(a_.h[:, n0:n0 + T], r.h[:, :T], AF.Exp, [r, dv], [a_], scale=dvl(l, "c1")[:, dr * 2 + jj:dr * 2 + jj + 1])
                    m1, m2 = tl.get(), tl.get()
                    ACT(m1.h[:, :T], r.h[:, :T], AF.Exp, [r, dv], [m1], scale=dvl(l, "c2")[:, dr * 2 + jj:dr * 2 + jj + 1])
                    ACT(m2.h[:, :T], m1.h[:, :T], AF.Ln, [m1], [m2], scale=-1.0, bias=1.0)
                    ACT(m1.h[:, :T], m2.h[:, :T], AF.Exp, [m2], [m1], scale=0.5)
                    TT("dve", m2.h[:, :T], ig.h[:, :T], xc.h[:, n0:n0 + T], ALU.mult, [ig, xc, m2], [m2])
                    TT("dve", u_.h[:, n0:n0 + T], m2.h[:, :T], m1.h[:, :T], ALU.mult, [m1, m2], [u_])
                    lstep[0] += 1
                    if modgen is not None and lstep[0] % 2 == 0:
                        next(modgen, None)
                if dr == 0:
                    P.op("dve", lambda e: e.tensor_tensor_scan(out=hs.h[:], data0=a_.h[:], data1=u_.h[:], initial=0.0,
                                                                op0=ALU.mult, op1=ALU.add), [a_, u_], [hs])
                else:
                    P.op("dve", lambda e: e.tensor_tensor_scan(out=hb.h[:, 0:NCTX][:, ::-1], data0=a_.h[:, 0:NCTX][:, ::-1],
                                                                data1=u_.h[:, 0:NCTX][:, ::-1], initial=0.0,
                                                                op0=ALU.mult, op1=ALU.add), [a_, u_], [hb])
                    P.op("dve", lambda e: e.tensor_tensor_scan(out=hb.h[:, NCTX:NT][:, ::-1], data0=a_.h[:, NCTX:NT][:, ::-1],
                                                                data1=u_.h[:, NCTX:NT][:, ::-1], initial=hb.h[:, 0:1],
                                                                op0=ALU.mult, op1=ALU.add), [a_, u_, hb], [hb])
                    TT("dve", hs.h[:], hs.h[:], hb.h[:], ALU.add, [hs, hb], [hs])
            TT("dve", ym.h[:], hs.h[:], gyb.h[:], ALU.mult, [hs, gyb], [ym])
            P.dma("sp", MX[6 + jj], ym.h[:], reads=[ym], writes=[dMX])
        if modgen is not None:
            for _ in modgen:
                pass
        bring.b, bring.i = banks, 0
        P.barrier()
        if stop_after == ("L", l):
            raise _Stop()
        AR.top = persist_top
        W3 = AR.alloc([8, 2 * DFF], BF16)
        W4 = AR.alloc([22, D], BF16)
        for k in range(8):
            P.dma("pool", W3.h[:, k, :], w3_d[l, :, k, :], writes=[W3])
        for k in range(0, 22, 2):
            P.dma("pool", W4.h[:, k:k + 2, :], w4_d[l, :, k:k + 2, :], writes=[W4])
        markW = AR.top
        krp = [AR.alloc([NT], BF16) for _ in range(2)]
        for kvh in range(2):
            MEMSET("dve", krp[kvh].h[:], 0.0, [krp[kvh]])
            P.dma("sp", krp[kvh].h[kvh * 64:(kvh + 1) * 64, :], FA[18][kvh * 64:(kvh + 1) * 64, :], reads=[dFA], writes=[krp[kvh]])
        va = AR.alloc([NB, 128], BF16)
        P.dma("sp", va.h[:], VT[2], reads=[dVT], writes=[va])
        qr_ = Ring([AR.alloc([4, 128], BF16) for _ in range(3)])
        er = Ring([AR.alloc([4, 128], BF16) for _ in range(12)])
        denr = Ring([AR.alloc([4, 128], F32) for _ in range(2)])
        recr = Ring([AR.alloc([4, 128], F32) for _ in range(2)])
        ysr = Ring([AR.alloc([4, 128], BF16) for _ in range(2)])
        mask2 = cmb.h[:, CM["mprev"]:CM["mprev"] + 256].rearrange("p (a b) -> p a b", b=128)
        pend = None
        for qb in range(NB):
            if qb < 2:
                keys = [(0, None), (1, None)]
            else:
                keys = [(0, None), (1, None)]
                if qb > 2:
                    keys.append((qb - 1, 0))
                keys.append((qb, None))
                if qb < NB - 1:
                    keys.append((qb + 1, 1))
            qt = qr_.get()
            P.dma("sp", qt.h[:], FA[14:18].rearrange("c p n -> p c n")[:, :, qb * 128:(qb + 1) * 128], reads=[dFA], writes=[qt])
            ys = ysr.get()
            for kvh in range(2):
                ph = slice(kvh * 64, (kvh + 1) * 64)
                ebs = []
                for (kb_, mk) in keys:
                    ps = bring.get()
                    MM(ps.h[:], [(krp[kvh].h[:, kb_ * 128:(kb_ + 1) * 128], qt.h[:, :, :])], [krp[kvh], qt], [ps])
                    eb = er.get()
                    ACT(eb.h[:], ps.h[:].rearrange("p (a b) -> p a b", b=128), AF.Exp, [ps], [eb], scale=0.125)
                    if mk is not None:
                        TT("dve", eb.h[:], eb.h[:], mask2[:, mk:mk + 1, :].to_broadcast([128, 4, 128]), ALU.mult, [eb, cmb], [eb])
                    ebs.append((kb_, eb))

                def finish(ebs=ebs, kvh=kvh, ph=ph, ys=ys, qb=qb, lastk=(kvh == 1)):
                    po, pd = bring.get(), bring.get()
                    MM(po.h[:], [(va.h[:, kb_, :], eb.h[:].rearrange("p a b -> p (a b)")) for kb_, eb in ebs], [va] + [eb for _, eb in ebs], [po])
                    MM(pd.h[:], [(cmB("ones"), eb.h[:].rearrange("p a b -> p (a b)")) for kb_, eb in ebs], [cmb] + [eb for _, eb in ebs], [pd])
                    den, rec = denr.get(), recr.get()
                    TT("dve", den.h[ph], pd.h[ph, :].rearrange("p (a b) -> p a b", b=128),
                       dvl(l, "esink")[ph, kvh * 4:(kvh + 1) * 4].unsqueeze(2).to_broadcast([64, 4, 128]), ALU.add, [pd, dv], [den])
                    ACT(den.h[ph], den.h[ph], AF.Ln, [den], [den])
                    ACT(rec.h[ph], den.h[ph], AF.Exp, [den], [rec], scale=-1.0)
                    TT("dve", ys.h[ph], po.h[ph, :].rearrange("p (a b) -> p a b", b=128), rec.h[ph], ALU.mult, [po, rec], [ys])
                    if lastk:
                        P.dma("sp", MX[2:6].rearrange("c p n -> p c n")[:, :, qb * 128:(qb + 1) * 128], ys.h[:], reads=[ys], writes=[dMX])
                if pend is not None:
                    pend()
                pend = finish
        pend()
        P.barrier()
        if stop_after == ("T", l):
            raise _Stop()
        AR.top = markW
        W2 = AR.alloc([8, D], BF16)
        P.dma("pool", W2.h[:], w2_d[l], writes=[W2])
        mxr = Ring([AR.alloc([8, 256], BF16) for _ in range(2)])
        xir = Ring([AR.alloc([8, 256], F32) for _ in range(2)])
        xor_ = Ring([AR.alloc([8, 256], F32) for _ in range(2)])
        for (n0, T) in [(i, 256) for i in range(0, NT, 256)]:
            j = 1 if n0 < NCTX else 0
            mt, xi, xo = mxr.get(), xir.get(), xor_.get()
            P.dma("sp", mt.h[:, :, :T], MX.rearrange("c p n -> p c n")[:, :, n0:n0 + T], reads=[dMX], writes=[mt])
            P.dma("act", xi.h[:, :, :T], src.rearrange("c p n -> p c n")[:, :, n0:n0 + T], reads=[dsrc], writes=[xi])
            for oc in range(8):
                pb = bring.get()
                MM(pb.h[:, :T], [(W2.h[:, k, oc * 128:(oc + 1) * 128], mt.h[:, k, :T]) for k in range(8)], [W2, mt], [pb])
                STT(xo.h[:, oc, :T], pb.h[:, :T], modcol(l, 16 + oc, j), xi.h[:, oc, :T], ALU.mult, ALU.add, [pb, xi, mv[l]], [xo])
            P.dma("pool", Xmid[0].rearrange("c p n -> p c n")[:, :, n0:n0 + T], xo.h[:, :, :T], reads=[xo], writes=[Xmid[1]])
        P.barrier()
        if stop_after == ("F1", l):
            raise _Stop()
        AR.top = markW
        LF = 256
        xwr = Ring([AR.alloc([8, LF + 2], F32) for _ in range(2)])
        sq2 = AR.alloc([8, LF + 2], BF16)
        ln2 = AR.alloc([LF + 2], F32)
        rs2 = AR.alloc([LF + 2], F32)
        hn2 = Ring([AR.alloc([LF + 2], F32) for _ in range(2)])
        h2 = AR.alloc([8, LF + 2], BF16)
        actb = AR.alloc([22, LF], BF16)
        agr = Ring([AR.alloc([LF], F32) for _ in range(2)])
        avr = Ring([AR.alloc([LF], F32) for _ in range(2)])
        sgr = Ring([AR.alloc([LF], F32) for _ in range(2)])
        xo2 = AR.alloc([8, LF], F32)
        fcw = pvl(l, "fcw")
        fcb = pvl(l, "fcb")
        ftiles = [(0, NCTX, s) for s in range(0, NCTX, LF)] + [(NCTX, NT, s) for s in range(NCTX, NT, LF)]
        h2r = Ring([h2, AR.alloc([8, LF + 2], BF16)])

        def prep(ft):
            g0, g1, s = ft
            j = 1 if s < NCTX else 0
            L = min(LF, g1 - s)
            wlo, whi = max(s - 1, g0), min(s + L + 1, g1)
            W = whi - wlo
            co = s - wlo
            xw = xwr.get()
            h2_ = h2r.get()
            P.dma("sp", xw.h[:, :, :W], Xmid[0].rearrange("c p n -> p c n")[:, :, wlo:whi], reads=[Xmid[1]], writes=[xw])
            rmsnorm_mod(l, xw, W, "gs2", 24, j, sq2, ln2, rs2, hn2, h2_)
            return (s, j, L, W, co, xw, h2_)

        nxt = prep(ftiles[0])
        for ti in range(len(ftiles)):
            s, j, L, W, co, xw, h2_ = nxt
            i0 = 1 - co
            i1 = min(L, W - 1 - co)
            for m in range(22):
                accs = []
                for (cidx, ring_) in ((m, agr), (22 + m, avr)):
                    pb = bring.get()
                    MM(pb.h[:, :W], [(W3.h[:, k, cidx * 128:(cidx + 1) * 128], h2_.h[:, k, :W]) for k in range(8)], [W3, h2_], [pb])
                    acc = ring_.get()
                    ACT(acc.h[:, :L], pb.h[:, co:co + L], AF.Identity, [pb, pv], [acc],
                        scale=fcw[:, cidx * 3 + 1:cidx * 3 + 2], bias=fcb[:, cidx:cidx + 1])
                    STT(acc.h[:, i0:L], pb.h[:, co + i0 - 1:co + L - 1], fcw[:, cidx * 3:cidx * 3 + 1], acc.h[:, i0:L],
                        ALU.mult, ALU.add, [pb, pv, acc], [acc])
                    STT(acc.h[:, 0:i1], pb.h[:, co + 1:co + 1 + i1], fcw[:, cidx * 3 + 2:cidx * 3 + 3], acc.h[:, 0:i1],
                        ALU.mult, ALU.add, [pb, pv, acc], [acc])
                    accs.append(acc)
                sgt = sgr.get()
                ACT(sgt.h[:, :L], accs[0].h[:, :L], AF.Silu, [accs[0]], [sgt])
                TT("dve", actb.h[:, m, :L], sgt.h[:, :L], accs[1].h[:, :L], ALU.mult, [sgt, accs[1]], [actb])
            if ti + 1 < len(ftiles):
                nxt = prep(ftiles[ti + 1])
            for oc in range(8):
                pb = bring.get()
                MM(pb.h[:, :L], [(W4.h[:, m, oc * 128:(oc + 1) * 128], actb.h[:, m, :L]) for m in range(22)], [W4, actb], [pb])
                STT(xo2.h[:, oc, :L], pb.h[:, :L], modcol(l, 40 + oc, j), xw.h[:, oc, co:co + L], ALU.mult, ALU.add, [pb, xw, mv[l]], [xo2])
            P.dma("sp", Xin[0].rearrange("c p n -> p c n")[:, :, s:s + L], xo2.h[:, :, :L], reads=[xo2], writes=[Xin[1]])
        P.barrier()
        if stop_after == ("F2", l):
            raise _Stop()

    def final_norm():
        AR.top = persist_top
        xzr = Ring([AR.alloc([8, 512], F32) for _ in range(2)])
        sqz = AR.alloc([8, 512], BF16)
        lnz = AR.alloc([512], F32)
        rsz = AR.alloc([512], F32)
        ozr = Ring([AR.alloc([8, 512], F32) for _ in range(2)])
        for i in range(8):
            n0 = NCTX + 512 * i
            xz, oz = xzr.get(), ozr.get()
            P.dma("sp", xz.h[:], Xin[0].rearrange("c p n -> p c n")[:, :, n0:n0 + 512], reads=[Xin[1]], writes=[xz])
            ACT(sqz.h[:], xz.h[:], AF.Square, [xz], [sqz])
            pb = bring.get()
            MM(pb.h[:], [(cmB("ones"), sqz.h[:, k, :]) for k in range(8)], [sqz, cmb], [pb])
            ACT(lnz.h[:], pb.h[:], AF.Ln, [pb], [lnz], scale=1.0 / D, bias=EPS)
            ACT(rsz.h[:], lnz.h[:], AF.Exp, [lnz], [rsz], scale=-0.5)
            for k in range(8):
                STT(oz.h[:, k, :], xz.h[:, k, :], pvg("fng")[:, k:k + 1], rsz.h[:], ALU.mult, ALU.mult, [xz, rsz, pv], [oz])
            P.dma("sp", outT[:, :, 512 * i:512 * (i + 1)].rearrange("c p n -> p c n"), oz.h[:], reads=[oz])
    try:
        for l in range(nlayers):
            layer_body(l)
        if nlayers == DEPTH:
            final_norm()
    except _Stop:
        pass
    P.build()
    return nc


def _fm(v, nchunk):
    return np.ascontiguousarray(np.asarray(v, np.float32).reshape(nchunk, 128).T)


def host_consts():
    cm = np.zeros((128, NCM), np.float32)
    i = np.arange(128)
    cm[:, CM["ident"]:CM["ident"] + 128] = np.eye(128)
    cm[:, CM["blkones"]:CM["blkones"] + 128] = (i[:, None] // 64 == i[None, :] // 64)
    same = (i[:, None] // CH == i[None, :] // CH)
    cm[:, CM["hgm_f"]:CM["hgm_f"] + 128] = same & (i[:, None] <= i[None, :])
    cm[:, CM["hgm_b"]:CM["hgm_b"] + 128] = same & (i[:, None] >= i[None, :])
    cm[:, CM["mprev"]:CM["mprev"] + 128] = (i[:, None] >= i[None, :])
    cm[:, CM["mnext"]:CM["mnext"] + 128] = (i[:, None] <= i[None, :])
    cm[:, CM["ones"]:CM["ones"] + 128] = 1.0
    t = np.arange(512)
    cm[:, CM["rm_f"]:CM["rm_f"] + 512] = (t % CH != 0)[None, :]
    cm[:, CM["rm_b"]:CM["rm_b"] + 512] = (t % CH != CH - 1)[None, :]
    cm[:, CM["chm"]:CM["chm"] + 4] = (i[:, None] // CH == np.arange(4)[None, :])
    nfreq = 16
    inv = (10000.0 ** (-np.arange(nfreq, dtype=np.float32) / nfreq)).astype(np.float32)
    pos = np.arange(NLAT)
    r = (pos // 64).astype(np.float32)
    c = (pos % 64).astype(np.float32)
    ang = np.concatenate([r[:, None] * inv, c[:, None] * inv], axis=-1).astype(np.float32)
    cos, sin = np.cos(ang).astype(np.float32), np.sin(ang).astype(np.float32)
    cs = np.zeros((2, 128, NT), np.float32)
    cs[0, :, :NCTX] = 1.0
    p = np.arange(128)
    cs[0, :, NCTX:] = cos.T[p % 32, :]
    sgn = np.where((p % 64) < 32, -1.0, 1.0).astype(np.float32)
    cs[1, :, NCTX:] = sin.T[p % 32, :] * sgn[:, None]
    return cm, cs


def w1_cols():
    cols = []
    cols += list(range(0, 256)) + list(range(256, 512)) + list(range(512, 768)) + list(range(1024, 1280))
    aq = 1280
    for jq in range(4):
        for h in (jq, 4 + jq):
            cols += list(range(aq + h * 64, aq + h * 64 + 64))
    for jq in range(4):
        for h in (jq, 4 + jq):
            cols += list(range(aq + h * 64 + 32, aq + h * 64 + 64)) + list(range(aq + h * 64, aq + h * 64 + 32))
    ak = 1792
    cols += list(range(ak, ak + 128))
    for h in range(2):
        cols += list(range(ak + h * 64 + 32, ak + h * 64 + 64)) + list(range(ak + h * 64, ak + h * 64 + 32))
    cols += list(range(2048, 2304)) + list(range(2304, 2560))
    cols += list(range(768, 1024)) + list(range(1920, 2048))
    assert len(cols) == NC1
    return np.array(cols)


def w2_rows():
    rows = list(range(0, 256))
    for g in range(4):
        for h in (g, 4 + g):
            rows += list(range(256 + h * 64, 256 + h * 64 + 64))
    rows += list(range(768, 1024))
    return np.array(rows)


def kmajor(w, nk):
    return np.ascontiguousarray(w.reshape(nk, 128, w.shape[-1]).transpose(1, 0, 2))


def prep_shared(inp):
    f = lambda k: np.asarray(inp[k], np.float32)
    cm, cs = host_consts()
    sh = {"cmask": cm, "cs": cs}
    sh["adaw"] = np.stack([kmajor(f("ada_w")[l], 8) for l in range(DEPTH)])
    c1 = w1_cols()
    sh["w1"] = np.stack([kmajor(f("w_in")[l][:, c1], 8) for l in range(DEPTH)])
    r2 = w2_rows()
    sh["w2"] = np.stack([kmajor(f("w_out")[l][r2, :], 8) for l in range(DEPTH)])
    sh["w3"] = np.stack([kmajor(f("ffn_w_up")[l], 8) for l in range(DEPTH)])
    sh["w4"] = np.stack([kmajor(f("ffn_w_down")[l], 22) for l in range(DEPTH)])
    wl = np.zeros((DEPTH, 128, 8, 128), np.float32)
    for l in range(DEPTH):
        for d in range(2):
            for gi, nm in enumerate(("lru_w_r", "lru_w_i")):
                w = f(nm)[l, d]
                for ch in range(2):
                    for hb in range(2):
                        wl[l, hb * 64:(hb + 1) * 64, d * 4 + gi * 2 + ch, hb * 64:(hb + 1) * 64] = w[ch * 2 + hb]
    sh["wlru"] = wl
    return sh


def prep_pv(inp, b):
    f = lambda k: np.asarray(inp[k], np.float32)
    pv = np.zeros((128, NPV), np.float32)
    cc = np.stack([_fm(f("c")[b], 8), _fm(f("c_ctx"), 8)], axis=-1)
    pv[:, 0:16] = cc.reshape(128, 16)
    pv[:, 16:24] = _fm(f("final_norm_g"), 8)
    lbr = np.stack([_fm(f("hg_lb_raw")[l], 2) for l in range(DEPTH)], axis=-1)
    pv[:, 24:32] = lbr.reshape(128, 8)
    for l in range(DEPTH):
        def put(name, arr):
            o, n = PV_L[name]
            o += PV_GN + l * PV_LN
            pv[:, o:o + n] = arr.reshape(128, n)
        put("gmix", _fm(f("norm_mix_g")[l], 8))
        put("gffn", _fm(f("norm_ffn_g")[l], 8))
        put("adab", _fm(f("ada_b")[l], 48))
        put("hgng", _fm(f("hg_norm_g")[l], 2))
        put("sink", np.broadcast_to(f("att_sink")[l][None, :], (128, 8)))
        put("lcw", np.stack([_fm(f("lru_conv_w")[l][t], 2) for t in range(4)], axis=-1))
        put("lcb", _fm(f("lru_conv_b")[l], 2))
        put("lbr", np.stack([_fm(f("lru_b_r")[l][d], 2) for d in range(2)], axis=1))
        put("lbi", np.stack([_fm(f("lru_b_i")[l][d], 2) for d in range(2)], axis=1))
        put("lam", np.stack([_fm(f("lru_lambda")[l][d], 2) for d in range(2)], axis=1))
        put("fcw", np.stack([_fm(f("ffn_conv_w")[l][t], 44) for t in range(3)], axis=-1))
        put("fcb", _fm(f("ffn_conv_b")[l], 44))
    return pv


def prep_x(inp, b):
    x = np.concatenate([np.asarray(inp["ctx"][b], np.float32), np.asarray(inp["x"][b], np.float32)], axis=0)
    return np.ascontiguousarray(x.T.reshape(8, 128, NT))


_CACHE = {}


def kernel(**inputs):
    nb = inputs["x"].shape[0]
    if "nc" not in _CACHE:
        _CACHE["nc"] = build_program()
    nc = _CACHE["nc"]
    sh = prep_shared(inputs)
    in_maps = []
    for b in range(nb):
        m = dict(sh)
        m["pv"] = prep_pv(inputs, b)
        m["xT0"] = prep_x(inputs, b)
        in_maps.append(m)
    res = run_bass_kernel_spmd(nc, in_maps, core_ids=list(range(nb)))
    out = np.empty((nb, NLAT, D), np.float32)
    for b in range(nb):
        o = np.asarray(res.results[b]["outT"], np.float32)
        out[b] = o.reshape(D, NLAT).T
    return out
```
